# Optimizing a Trainium2 kernel written in Bass

```python
import math
import jax, jax.numpy as jnp
from jax import lax
import numpy as np

D_MODEL = 1024
BATCH = 4
SEQ = 4096
DEPTH = 2

N_MEM = 256
N_BRANCH = 3
BRANCH_W = D_MODEL // 2
LRU_BLOCKS = 4
LRU_BLOCK = BRANCH_W // LRU_BLOCKS
CONV_W = 4
LRU_C = 8.0
GLA_HEADS = 4
GLA_DV = BRANCH_W // GLA_HEADS
GLA_DK = GLA_DV // 2
GLA_KW = GLA_HEADS * GLA_DK
GLA_RANK = 16
GLA_TAU = 16.0
GLA_CHUNK = 64
S5_GROUP = 16
S5_GROUPS = BRANCH_W // S5_GROUP
S5_STATE = 64
XA_HEADS = 4
XA_HD = D_MODEL // XA_HEADS
D_FF = -(-8 * D_MODEL // (3 * 256)) * 256
DN_ALPHA = (2 * DEPTH) ** 0.25
DN_BETA = (8 * DEPTH) ** -0.25
LN_EPS = 1e-5

IN_SIZES = (BRANCH_W, BRANCH_W,
            GLA_KW, GLA_KW, BRANCH_W,
            BRANCH_W, GLA_RANK,
            BRANCH_W,
            N_BRANCH * D_MODEL)
D_IN = sum(IN_SIZES)
IN_SPLITS = tuple(int(s) for s in np.cumsum(IN_SIZES)[:-1])

kernel_name = 'hybrid_rglru_gla_s5_deepnorm'


def layer_norm(x, g, b):
    xf = x.astype(jnp.float32)
    mu = jnp.mean(xf, axis=-1, keepdims=True)
    var = jnp.mean(jnp.square(xf - mu), axis=-1, keepdims=True)
    return ((xf - mu) * lax.rsqrt(var + LN_EPS)).astype(x.dtype) * g + b


def rms_norm(x, g):
    xf = x.astype(jnp.float32)
    return (xf * lax.rsqrt(jnp.mean(xf * xf, axis=-1, keepdims=True) + LN_EPS)).astype(x.dtype) * g


def causal_dwconv(u, w, b):
    L = u.shape[1]
    up = jnp.pad(u, ((0, 0), (CONV_W - 1, 0), (0, 0)))
    out = b + up[:, 0:L] * w[0]
    for k in range(1, CONV_W):
        out = out + up[:, k:k + L] * w[k]
    return out


def _linear_combine(e1, e2):
    a1, b1 = e1
    a2, b2 = e2
    return a1 * a2, a2 * b1 + b2


def rg_lru(u, w_a, b_a, w_i, b_i, lam):
    B_, L, _ = u.shape
    ub = u.reshape(B_, L, LRU_BLOCKS, LRU_BLOCK)
    r = jax.nn.sigmoid(jnp.einsum('blhi,hij->blhj', ub, w_a).reshape(B_, L, BRANCH_W) + b_a)
    i = jax.nn.sigmoid(jnp.einsum('blhi,hij->blhj', ub, w_i).reshape(B_, L, BRANCH_W) + b_i)
    log_a = -LRU_C * r * jax.nn.softplus(-lam)
    a = jnp.exp(log_a)
    mult = jnp.sqrt(-jnp.expm1(2.0 * log_a))
    _, h = lax.associative_scan(_linear_combine, (a, mult * (i * u)), axis=1)
    return h


def gla_chunked(q, k, v, log_f):
    B_, L = q.shape[0], q.shape[1]
    NC = L // GLA_CHUNK

    def chunks(t):
        return t.reshape(B_, NC, GLA_CHUNK, GLA_HEADS, t.shape[-1]).transpose(0, 3, 1, 2, 4)

    q, k, v, g = chunks(q) * GLA_DK ** -0.5, chunks(k), chunks(v), chunks(log_f)
    b = jnp.cumsum(g, axis=3)
    b_last = b[:, :, :, -1:]
    q_dec = q * jnp.exp(b)
    k_inv = k * jnp.exp(-b)
    k_end = k * jnp.exp(b_last - b)
    causal = jnp.tril(jnp.ones((GLA_CHUNK, GLA_CHUNK), dtype=bool))
    att = jnp.where(causal, jnp.einsum('bhnck,bhnsk->bhncs', q_dec, k_inv), 0.0)
    o_intra = jnp.einsum('bhncs,bhnsv->bhncv', att, v)
    d_state = jnp.einsum('bhnsk,bhnsv->bhnkv', k_end, v)
    decay = jnp.exp(b_last[:, :, :, 0])

    def step(S, inp):
        dec, ds = inp
        return dec[..., None] * S + ds, S

    S0 = jnp.zeros((B_, GLA_HEADS, GLA_DK, GLA_DV), q.dtype)
    _, S_prev = lax.scan(step, S0, (jnp.moveaxis(decay, 2, 0), jnp.moveaxis(d_state, 2, 0)))
    S_prev = jnp.moveaxis(S_prev, 0, 2)
    o = o_intra + jnp.einsum('bhnck,bhnkv->bhncv', q_dec, S_prev)
    return o.transpose(0, 2, 3, 1, 4).reshape(B_, L, GLA_HEADS, GLA_DV)


def s5_ssm(u, lam_re, lam_im, log_dt, b_re, b_im, c_re, c_im, d_skip):
    B_, L, _ = u.shape
    dt = jnp.exp(log_dt)[:, None]
    mag = jnp.exp(lam_re * dt)
    ab_re = mag * jnp.cos(lam_im * dt)
    ab_im = mag * jnp.sin(lam_im * dt)
    den = lam_re * lam_re + lam_im * lam_im
    f_re = ((ab_re - 1.0) * lam_re + ab_im * lam_im) / den
    f_im = (ab_im * lam_re - (ab_re - 1.0) * lam_im) / den
    bb_re = f_re[..., None] * b_re - f_im[..., None] * b_im
    bb_im = f_re[..., None] * b_im + f_im[..., None] * b_re
    ug = u.reshape(B_, L, S5_GROUPS, S5_GROUP)
    bu_re = jnp.einsum('blgi,gpi->blgp', ug, bb_re)
    bu_im = jnp.einsum('blgi,gpi->blgp', ug, bb_im)
    a_re = jnp.broadcast_to(ab_re, (1, L, S5_GROUPS, S5_STATE))
    a_im = jnp.broadcast_to(ab_im, (1, L, S5_GROUPS, S5_STATE))

    def combine(e1, e2):
        a1r, a1i, b1r, b1i = e1
        a2r, a2i, b2r, b2i = e2
        return (a2r * a1r - a2i * a1i, a2r * a1i + a2i * a1r,
                a2r * b1r - a2i * b1i + b2r, a2r * b1i + a2i * b1r + b2i)

    _, _, xr, xi = lax.associative_scan(combine, (a_re, a_im, bu_re, bu_im), axis=1)
    y = jnp.einsum('blgp,gip->blgi', xr, c_re) - jnp.einsum('blgp,gip->blgi', xi, c_im)
    return y.reshape(B_, L, BRANCH_W) + d_skip * u


def hybrid_mixer(h, w_in, b_in, conv_w, conv_b, lru_w_a, lru_b_a, lru_w_i, lru_b_i, lru_lambda,
                 gla_w_lr, gla_b_lr, gla_norm_g,
                 s5_lam_re, s5_lam_im, s5_log_dt, s5_b_re, s5_b_im, s5_c_re, s5_c_im, s5_d,
                 s5_w_glu, s5_b_glu, w_branch, w_mix_out):
    B_, L, _ = h.shape
    proj = h @ w_in + b_in
    u_lru, g_lru, q, k, v, og, lr, u_s5, gate_logits = jnp.split(proj, IN_SPLITS, axis=-1)
    y_a = rg_lru(causal_dwconv(u_lru, conv_w, conv_b), lru_w_a, lru_b_a, lru_w_i, lru_b_i,
                 lru_lambda) * jax.nn.gelu(g_lru)
    log_f = jax.nn.log_sigmoid(lr @ gla_w_lr + gla_b_lr) / GLA_TAU
    o = gla_chunked(q.reshape(B_, L, GLA_HEADS, GLA_DK), k.reshape(B_, L, GLA_HEADS, GLA_DK),
                    v.reshape(B_, L, GLA_HEADS, GLA_DV), log_f.reshape(B_, L, GLA_HEADS, GLA_DK))
    y_b = rms_norm(o, gla_norm_g).reshape(B_, L, BRANCH_W) * jax.nn.silu(og)
    z = jax.nn.gelu(s5_ssm(u_s5, s5_lam_re, s5_lam_im, s5_log_dt, s5_b_re, s5_b_im,
                           s5_c_re, s5_c_im, s5_d))
    y_c = z * jax.nn.sigmoid(z @ s5_w_glu + s5_b_glu)
    ys = jnp.stack([y_a, y_b, y_c], axis=2)
    gates = jax.nn.sigmoid(gate_logits.reshape(B_, L, N_BRANCH, D_MODEL))
    merged = jnp.sum(gates * jnp.einsum('blnw,nwd->blnd', ys, w_branch), axis=2)
    return merged @ w_mix_out


def cross_attention(h, mem, w_q, w_kv, w_o):
    B_, L, _ = h.shape
    q = (h @ w_q).reshape(B_, L, XA_HEADS, XA_HD)
    k, v = jnp.split(mem @ w_kv, 2, axis=-1)
    k = k.reshape(B_, -1, XA_HEADS, XA_HD)
    v = v.reshape(B_, -1, XA_HEADS, XA_HD)
    s = jnp.einsum('blhd,bmhd->bhlm', q, k) * XA_HD ** -0.5
    p = jax.nn.softmax(s.astype(jnp.float32), axis=-1).astype(v.dtype)
    o = jnp.einsum('bhlm,bmhd->blhd', p, v).reshape(B_, L, D_MODEL)
    return o @ w_o


def swiglu(h, w_gu, w_down):
    gate, up = jnp.split(h @ w_gu, 2, axis=-1)
    return (jax.nn.silu(gate) * up) @ w_down


def setup_inputs(seed: int = 0) -> dict:
    key = jax.random.key(seed)
    ks = iter(jax.random.split(key, 64))

    def nrm(shape, scale):
        return jax.random.normal(next(ks), shape, jnp.float32) * scale

    def gain(shape):
        return 1.0 + nrm(shape, 0.01)

    N = DEPTH
    u = jax.random.uniform(next(ks), (N, BRANCH_W), jnp.float32, 0.9, 0.999)
    s = u ** (1.0 / LRU_C)
    lru_lambda = jnp.log(s) - jnp.log1p(-s)
    n_idx = jnp.arange(S5_STATE, dtype=jnp.float32)
    log_dt = jax.random.uniform(next(ks), (N, S5_GROUPS), jnp.float32,
                                math.log(1e-3), math.log(1e-1))
    return {
        'x': nrm((BATCH, SEQ, D_MODEL), 1.0),
        'mem': nrm((BATCH, N_MEM, D_MODEL), 1.0),
        'ln_in_g': gain((D_MODEL,)),
        'ln_in_b': nrm((D_MODEL,), 0.01),
        'w_in': nrm((N, D_MODEL, D_IN), D_MODEL ** -0.5),
        'b_in': nrm((N, D_IN), 0.01),
        'lru_conv_w': nrm((N, CONV_W, BRANCH_W), CONV_W ** -0.5),
        'lru_conv_b': nrm((N, BRANCH_W), 0.01),
        'lru_w_a': nrm((N, LRU_BLOCKS, LRU_BLOCK, LRU_BLOCK), LRU_BLOCK ** -0.5),
        'lru_b_a': nrm((N, BRANCH_W), 0.01),
        'lru_w_i': nrm((N, LRU_BLOCKS, LRU_BLOCK, LRU_BLOCK), LRU_BLOCK ** -0.5),
        'lru_b_i': nrm((N, BRANCH_W), 0.01),
        'lru_lambda': lru_lambda,
        'gla_w_lr': nrm((N, GLA_RANK, GLA_KW), GLA_RANK ** -0.5),
        'gla_b_lr': nrm((N, GLA_KW), 0.01),
        'gla_norm_g': gain((N, GLA_DV)),
        's5_lam_re': -0.5 + nrm((N, S5_GROUPS, S5_STATE), 0.01),
        's5_lam_im': math.pi * n_idx + nrm((N, S5_GROUPS, S5_STATE), 0.01),
        's5_log_dt': log_dt,
        's5_b_re': nrm((N, S5_GROUPS, S5_STATE, S5_GROUP), (2 * S5_GROUP) ** -0.5),
        's5_b_im': nrm((N, S5_GROUPS, S5_STATE, S5_GROUP), (2 * S5_GROUP) ** -0.5),
        's5_c_re': nrm((N, S5_GROUPS, S5_GROUP, S5_STATE), (2 * S5_STATE) ** -0.5),
        's5_c_im': nrm((N, S5_GROUPS, S5_GROUP, S5_STATE), (2 * S5_STATE) ** -0.5),
        's5_d': nrm((N, BRANCH_W), 1.0),
        's5_w_glu': nrm((N, BRANCH_W, BRANCH_W), BRANCH_W ** -0.5),
        's5_b_glu': nrm((N, BRANCH_W), 0.01),
        'w_branch': nrm((N, N_BRANCH, BRANCH_W, D_MODEL), BRANCH_W ** -0.5),
        'w_mix_out': nrm((N, D_MODEL, D_MODEL), D_MODEL ** -0.5 * DN_BETA),
        'ln1_g': gain((N, D_MODEL)),
        'ln1_b': nrm((N, D_MODEL), 0.01),
        'xa_w_q': nrm((N, D_MODEL, D_MODEL), D_MODEL ** -0.5),
        'xa_w_kv': nrm((N, D_MODEL, 2 * D_MODEL), D_MODEL ** -0.5),
        'xa_w_o': nrm((N, D_MODEL, D_MODEL), D_MODEL ** -0.5 * DN_BETA),
        'ln2_g': gain((N, D_MODEL)),
        'ln2_b': nrm((N, D_MODEL), 0.01),
        'ffn_w_gu': nrm((N, D_MODEL, 2 * D_FF), D_MODEL ** -0.5),
        'ffn_w_down': nrm((N, D_FF, D_MODEL), D_FF ** -0.5 * DN_BETA),
        'ln3_g': gain((N, D_MODEL)),
        'ln3_b': nrm((N, D_MODEL), 0.01),
    }


def reference(x, mem, ln_in_g, ln_in_b, w_in, b_in, lru_conv_w, lru_conv_b, lru_w_a, lru_b_a,
              lru_w_i, lru_b_i, lru_lambda, gla_w_lr, gla_b_lr, gla_norm_g,
              s5_lam_re, s5_lam_im, s5_log_dt, s5_b_re, s5_b_im, s5_c_re, s5_c_im, s5_d,
              s5_w_glu, s5_b_glu, w_branch, w_mix_out, ln1_g, ln1_b,
              xa_w_q, xa_w_kv, xa_w_o, ln2_g, ln2_b, ffn_w_gu, ffn_w_down, ln3_g, ln3_b):
    h = layer_norm(x, ln_in_g, ln_in_b)
    for l in range(DEPTH):
        mix = hybrid_mixer(h, w_in[l], b_in[l], lru_conv_w[l], lru_conv_b[l], lru_w_a[l],
                           lru_b_a[l], lru_w_i[l], lru_b_i[l], lru_lambda[l],
                           gla_w_lr[l], gla_b_lr[l], gla_norm_g[l],
                           s5_lam_re[l], s5_lam_im[l], s5_log_dt[l], s5_b_re[l], s5_b_im[l],
                           s5_c_re[l], s5_c_im[l], s5_d[l], s5_w_glu[l], s5_b_glu[l],
                           w_branch[l], w_mix_out[l])
        h = layer_norm(DN_ALPHA * h + mix, ln1_g[l], ln1_b[l])
        h = layer_norm(DN_ALPHA * h + cross_attention(h, mem, xa_w_q[l], xa_w_kv[l], xa_w_o[l]),
                       ln2_g[l], ln2_b[l])
        h = layer_norm(DN_ALPHA * h + swiglu(h, ffn_w_gu[l], ffn_w_down[l]), ln3_g[l], ln3_b[l])
    return h
```

```python
import math
import contextlib
import numpy as np
import concourse.bass as bass
import concourse.mybir as mybir
from concourse.bass_utils import run_bass_kernel_spmd

F32 = mybir.dt.float32
BF16 = mybir.dt.bfloat16
AF = mybir.ActivationFunctionType
ALU = mybir.AluOpType

D = 1024
TT = 512
NMEM = 256
DEPTH = 2
LAG = 1
NL = 1
D_IN = 6160
DFF = 2816
ALPHA = (2 * DEPTH) ** 0.25
EPS = 1e-5
PI = math.pi
SAME_ENG_SYNC = True
RELAX_SELF = True
SENT = 1 << 40
MAGIC = 12582912.0
TWO_PI_S = 2 * math.pi * (1 - 2e-7)


class V:
    __slots__ = ("buf", "ap")

    def __init__(self, buf, ap):
        self.buf = buf
        self.ap = ap


class Buf:
    def __init__(self, t, name):
        self.t = t
        self.name = name
        self.w = None
        self.rs = []
        self.dsem = None

    def __getitem__(self, idx):
        return V(self, self.t[idx])


class Node:
    __slots__ = ("idx", "eng", "fn", "deps", "succ", "dur", "kind", "sembuf", "nbytes", "nun", "ready",
                 "start", "end", "sem", "val", "clk", "done", "tag", "inc", "afam")

    def __init__(self, idx, eng, fn, deps, dur, kind, sembuf=None, nbytes=0):
        self.idx = idx; self.eng = eng; self.fn = fn; self.deps = deps; self.succ = []
        self.dur = dur; self.kind = kind; self.sembuf = sembuf; self.nbytes = nbytes
        self.nun = 0; self.ready = 0.0; self.start = None; self.end = None
        self.sem = None; self.val = None; self.clk = None; self.done = False; self.inc = 16; self.afam = None


ENGS = ("tensor", "vector", "scalar", "gpsimd", "sync")
AFAM = {AF.Exp: "lnexp", AF.Ln: "lnexp", AF.Sigmoid: "sig", AF.Sqrt: "sqrt", AF.Sin: "sin", AF.Gelu: "gelu", AF.Silu: "silu"}
ATL = 1.28
SELFSYNC = {"tensor": False, "vector": SAME_ENG_SYNC, "scalar": SAME_ENG_SYNC, "gpsimd": SAME_ENG_SYNC, "sync": False}
DMA_BW = 150e3
WINDOW = 64


class Ctx:
    def __init__(self, nc, stack):
        self.nc = nc
        self.stack = stack
        self.sems = {nm: stack.enter_context(nc.semaphore("sem_" + nm)) for nm in ENGS}
        self.nodes = []
        self.tag = "setup"
        self.itn = -1
        self.nbuf = 0
        self.setup_sem = None
        self.setup_nodes = []
        self.groups = []
        self.final = None

    def sbuf(self, shape, dtype, name=None):
        self.nbuf += 1
        name = name or f"b{self.nbuf}"
        t = self.stack.enter_context(self.nc.sbuf_tensor(name, list(shape), dtype))
        return Buf(t, name)

    def psum(self, name):
        t = self.stack.enter_context(self.nc.psum_tensor(name, [128, 512], F32))
        return Buf(t, name)

    def dram(self, name, shape, dtype, kind="Internal"):
        t = self.nc.dram_tensor(name, list(shape), dtype, kind=kind)
        b = Buf(t, name)
        b.apv = t.ap()
        return b

    def _mk(self, eng, fn, reads, writes, dur, kind, sembuf=None, nbytes=0, group=None):
        rb = [x.buf if isinstance(x, V) else x for x in reads]
        wb = [x.buf if isinstance(x, V) else x for x in writes]
        deps = set()
        for b in rb:
            if b.w is not None:
                deps.add(b.w)
        if group is not None and "wdeps" in group:
            deps |= group["wdeps"]
        else:
            wd = set()
            for b in wb:
                for r in b.rs:
                    wd.add(r)
                if b.w is not None:
                    wd.add(b.w)
            if group is not None:
                group["wdeps"] = wd
            deps |= wd
        n = Node(len(self.nodes), eng, fn, deps, dur, kind, sembuf, nbytes)
        n.tag = self.tag + "@" + str(self.itn)
        deps.discard(n)
        self.nodes.append(n)
        for b in rb:
            b.rs.append(n)
        for b in wb:
            b.w = n
            b.rs = []
        return n

    def op(self, engname, fn, reads, writes, dur=0.6):
        return self._mk(engname, fn, reads, writes, dur, "op")

    def dma(self, qname, fn, reads, writes, sembuf, nbytes=1 << 20, group=None, inc=16):
        if sembuf.dsem is None:
            sembuf.dsem = self.stack.enter_context(self.nc.semaphore("ds_" + sembuf.name))
        n = self._mk(qname, fn, reads, writes, 0.1, "dma", sembuf, nbytes, group)
        n.inc = inc
        if group is not None:
            group.setdefault("nodes", []).append(n)
            self.groups.append(group) if group.get("reg") is None else None
            group["reg"] = True
        return n

    def dma_setup(self, fn, out):
        if self.setup_sem is None:
            self.setup_sem = self.stack.enter_context(self.nc.semaphore("ds_setup"))
        n = Node(len(self.nodes), "sync", fn, set(), 0.1, "setup", None, 4096)
        n.tag = "setup"
        self.nodes.append(n)
        self.setup_nodes.append(n)
        out.buf.w = n
        out.buf.rs = []
        return n

    def final_wait(self, qname, toks):
        self.final = (qname, toks)

    def schedule(self):
        nodes = self.nodes
        for n in nodes:
            n.nun = len(n.deps)
            for d in n.deps:
                d.succ.append(n)
        queues = {e: [n for n in nodes if n.eng == e] for e in ENGS}
        pos = {e: 0 for e in ENGS}
        win = {e: [] for e in ENGS}
        nxt = {e: 0 for e in ENGS}
        tfree = {e: 0.0 for e in ENGS}
        dma_free = 0.0
        order = []
        total = len(nodes)
        cur_fam = [None]

        def refill(e):
            q = queues[e]
            w = win[e]
            while len(w) < WINDOW and nxt[e] < len(q):
                w.append(q[nxt[e]])
                nxt[e] += 1

        for e in ENGS:
            refill(e)
        while len(order) < total:
            best = None
            bs = None
            for e in ENGS:
                tf = tfree[e]
                for n in win[e]:
                    if n.nun == 0:
                        st = n.ready if n.ready > tf else tf
                        if n.afam is not None and n.afam != cur_fam[0]:
                            st += ATL
                        if bs is None or st < bs or (st == bs and n.idx < best.idx):
                            bs = st
                            best = n
            assert best is not None, "scheduler deadlock (cyclic deps?)"
            n = best
            e = n.eng
            if n.afam is not None:
                cur_fam[0] = n.afam
            n.start = bs
            if n.kind == "op":
                n.end = bs + n.dur
                tfree[e] = n.end
            else:
                tfree[e] = bs + n.dur
                xs = dma_free if dma_free > bs else bs
                dma_free = xs + n.nbytes / DMA_BW
                n.end = dma_free + 2.0
            win[e].remove(n)
            refill(e)
            for sn in n.succ:
                sn.nun -= 1
                if n.end > sn.ready:
                    sn.ready = n.end
            order.append(n)
        self.order = order
        self.est_us = max(n.end for n in nodes)

    def emit(self):
        order = self.order
        cnt = {e: 0 for e in ENGS}
        dcnt = {}
        n_setup = len(self.setup_nodes)
        selfbig = {}
        for n in order:
            e = n.eng
            if n.kind == "op":
                cnt[e] += 1
                n.sem = self.sems[e]
                n.val = cnt[e]
                n.inc = 1
                selfbig.setdefault(e, {})[n.val] = n.dur >= 0.5
            elif n.kind == "dma":
                sb = n.sembuf
                dcnt[id(sb)] = dcnt.get(id(sb), 0) + 1
                n.sem = sb.dsem
                n.val = n.inc * dcnt[id(sb)]
            else:
                n.sem = self.setup_sem
                n.val = 16 * n_setup
                n.inc = 16
        for g in self.groups:
            gv = max(m.val for m in g["nodes"])
            for m in g["nodes"]:
                m.val = gv
        known = {e: {} for e in ENGS}
        streams = {e: [] for e in ENGS}
        engsem = {id(self.sems[e]) for e in ENGS}
        for n in order:
            e = n.eng
            kn = known[e]
            need = {}
            for d in n.deps:
                k = id(d.sem)
                if k not in need or need[k][1] < d.val:
                    need[k] = (d.sem, d.val, d.clk)
            waits = []
            for k, (sem, val, clk) in need.items():
                if kn.get(k, 0) >= val:
                    continue
                if sem is self.sems[e]:
                    if not SELFSYNC[e]:
                        continue
                    if RELAX_SELF and e in ("vector", "scalar") and n.dur >= 0.5 and selfbig.get(e, {}).get(val, False):
                        continue
                waits.append((sem, val))
                kn[k] = val
                if clk:
                    for kk, vv in clk.items():
                        if kn.get(kk, 0) < vv:
                            kn[kk] = vv
            if n.kind == "op":
                clk = {k: v for k, v in kn.items() if k in engsem}
                clk[id(n.sem)] = n.val
                n.clk = clk
            else:
                n.clk = None
            streams[e].append((waits, n.fn, n.sem, n.inc))
        if self.final is not None:
            q, toks = self.final
            streams[q].append(([(t.sem, t.val) for t in toks], None, None, 0))
        self.streams = streams

    def replay(self, engname, e):
        for (waits, fn, sem, inc) in self.streams[engname]:
            for (s, v) in waits:
                e.wait_ge(s, v)
            if fn is not None:
                fn(e).then_inc(sem, inc)


class Pool:
    def __init__(self, bufs):
        self.free = list(bufs)
        self.all = list(bufs)

    def get(self):
        assert self.free, "pool exhausted"
        return self.free.pop(0)

    def put(self, b):
        assert b in self.all and b not in self.free
        self.free.append(b)


def build(n_tiles=8, n_stage=7, ncores=8):
    nc = bass.Bass("TRN2", target_bir_lowering=False)
    L = n_tiles * TT
    stack = contextlib.ExitStack()
    with stack:
        cx = Ctx(nc, stack)
        _build_body(nc, cx, n_tiles, n_stage, L, [[2 * i, 2 * i + 1] for i in range(ncores // 2)])
        cx.schedule()
        cx.emit()
        build.last_est_us = cx.est_us
        with nc.Block() as block:
            @block.tensor
            def _(e):
                cx.replay("tensor", e)

            @block.vector
            def _(e):
                cx.replay("vector", e)

            @block.scalar
            def _(e):
                cx.replay("scalar", e)

            @block.gpsimd
            def _(e):
                cx.replay("gpsimd", e)

            @block.sync
            def _(e):
                cx.replay("sync", e)
    return nc


def _build_body(nc, cx, n_tiles, n_stage, L, RG):
    def din(name, shape):
        b = cx.dram(name, shape, F32, kind="ExternalInput")
        return b

    x_d = din("x", [L, D])
    mem_d = din("mem", [NMEM, D])
    ln_in_g = din("ln_in_g", [D]); ln_in_b = din("ln_in_b", [D])
    w_in = din("w_in", [NL, D, D_IN]); b_in = din("b_in", [NL, D_IN])
    conv_w = din("lru_conv_w", [NL, 4, 512]); conv_b = din("lru_conv_b", [NL, 512])
    lru_w_a = din("lru_w_a", [NL, 4, 128, 128]); lru_b_a = din("lru_b_a", [NL, 512])
    lru_w_i = din("lru_w_i", [NL, 4, 128, 128]); lru_b_i = din("lru_b_i", [NL, 512])
    lru_lam = din("lru_lambda", [NL, 512])
    gla_w_lr = din("gla_w_lr", [NL, 16, 256]); gla_b_lr = din("gla_b_lr", [NL, 256])
    gla_g = din("gla_norm_g", [NL, 128])
    s5_lre = din("s5_lam_re", [NL, 32, 64]); s5_lim = din("s5_lam_im", [NL, 32, 64])
    s5_ldt = din("s5_log_dt", [NL, 32])
    s5_bre = din("s5_b_re", [NL, 32, 64, 16]); s5_bim = din("s5_b_im", [NL, 32, 64, 16])
    s5_cre = din("s5_c_re", [NL, 32, 16, 64]); s5_cim = din("s5_c_im", [NL, 32, 16, 64])
    s5_d = din("s5_d", [NL, 512]); s5_wglu = din("s5_w_glu", [NL, 512, 512]); s5_bglu = din("s5_b_glu", [NL, 512])
    w_branch = din("w_branch", [NL, 3, 512, D]); w_mix = din("w_mix_out", [NL, D, D])
    ln1_g = din("ln1_g", [NL, D]); ln1_b = din("ln1_b", [NL, D])
    xa_wq = din("xa_w_q", [NL, D, D]); xa_wkv = din("xa_w_kv", [NL, D, 2 * D]); xa_wo = din("xa_w_o", [NL, D, D])
    ln2_g = din("ln2_g", [NL, D]); ln2_b = din("ln2_b", [NL, D])
    w_gu = din("ffn_w_gu", [NL, D, 2 * DFF]); w_down = din("ffn_w_down", [NL, DFF, D])
    ln3_g = din("ln3_g", [NL, D]); ln3_b = din("ln3_b", [NL, D])
    role_d = din("role", [128, 1])
    out_d = cx.dram("out", [L, D], F32, kind="ExternalOutput")
    snd_d = [cx.dram(f"snd{i}", [128, 8 * TT], F32, kind="Internal") for i in range(3)]
    gat_d = [cx.dram(f"gat{i}", [256, 8 * TT], F32, kind="Internal") for i in range(2)]

    def scr(name, shape):
        return cx.dram(name, shape, BF16, kind="Internal")

    w_in_s = scr("w_in_s", [NL, D, D_IN])
    wa_s = scr("wa_s", [NL, 512, 128]); wi_s = scr("wi_s", [NL, 512, 128])
    wglu_s = scr("wglu_s", [NL, 512, 512])
    wbr_s = scr("wbr_s", [NL, 1536, D]); wmix_s = scr("wmix_s", [NL, D, D])
    wq_s = scr("wq_s", [NL, D, D]); wo_s = scr("wo_s", [NL, D, D])
    wgu_s = scr("wgu_s", [NL, D, 2 * DFF]); wdn_s = scr("wdn_s", [NL, DFF, D])
    s5w_s = scr("s5w_s", [NL, 128, 80 * 128])
    tab_s = cx.dram("tab_s", [NL, 16, 128, 2 * TT], F32, kind="Internal")
    kt_s = scr("kt_s", [NL, 128, 8 * 256])
    v_s = scr("v_s", [NL, 128, 2 * 1024])

    dve = "vector"; act = "scalar"; pool = "gpsimd"; pe = "tensor"; sp = "sync"

    def nel(ap):
        n = 1
        for d_ in ap.shape[1:]:
            n *= d_
        return n

    def is_ps(v):
        return v.buf.name.startswith("ps")

    def edur(eng, n, psrc=False, mult=1.0):
        if eng == dve:
            return (0.12 + n / 700.0 * (1.15 if psrc else 1.0)) * mult
        if eng == pool:
            return (0.3 + n / 420.0) * mult
        return (0.25 + n / 1350.0) * mult

    def mm(out, lhsT, rhs, start, stop):
        cols = nel(rhs.ap)
        d_ = 0.004 + max(cols, 64) / 2400.0 * (4 if rhs.ap.dtype == F32 else 1)
        cx.op(pe, lambda e: e.matmul(out.ap, lhsT.ap, rhs.ap, start=start, stop=stop), [lhsT, rhs], [out], d_)

    def transpose(out, in_, ident):
        cx.op(pe, lambda e: e.transpose(out.ap, in_.ap, ident.ap), [in_, ident], [out], 0.31)

    def actf(out, in_, func, bias=None, scale=None, eng=act):
        kw = {}
        rd = [in_]
        if bias is not None:
            if isinstance(bias, V):
                kw["bias"] = bias.ap; rd.append(bias)
            else:
                kw["bias"] = float(bias)
        if scale is not None:
            if isinstance(scale, V):
                kw["scale"] = scale.ap; rd.append(scale)
            else:
                kw["scale"] = float(scale)
        nd = cx.op(act, lambda e: e.activation(out=out.ap, in_=in_.ap, func=func, **kw), rd, [out], edur(act, nel(out.ap)))
        nd.afam = AFAM.get(func)

    def tt(eng, out, in0, in1, op):
        cx.op(eng, lambda e: e.tensor_tensor(out=out.ap, in0=in0.ap, in1=in1.ap, op=op), [in0, in1], [out],
              edur(eng, nel(out.ap), is_ps(in0) or is_ps(in1)))

    def ts(eng, out, in0, s1, s2, op0, op1=None):
        rd = [in0]
        a1 = s1
        if isinstance(s1, V):
            rd.append(s1); a1 = s1.ap
        a2 = s2
        if isinstance(s2, V):
            rd.append(s2); a2 = s2.ap
        d_ = edur(eng, nel(out.ap), is_ps(in0), 5.5 if eng == pool else 1.0)
        if op1 is None:
            cx.op(eng, lambda e: e.tensor_scalar(out=out.ap, in0=in0.ap, scalar1=a1, scalar2=None, op0=op0), rd, [out], d_)
        else:
            cx.op(eng, lambda e: e.tensor_scalar(out=out.ap, in0=in0.ap, scalar1=a1, scalar2=a2, op0=op0, op1=op1), rd, [out], d_)

    def stt(eng, out, in0, scalar, in1, op0, op1):
        eng = dve
        rd = [in0, in1]
        a = scalar
        if isinstance(scalar, V):
            rd.append(scalar); a = scalar.ap
        cx.op(eng, lambda e: e.scalar_tensor_tensor(out=out.ap, in0=in0.ap, scalar=a, in1=in1.ap, op0=op0, op1=op1), rd, [out],
              edur(dve, nel(out.ap), is_ps(in0) or is_ps(in1)))

    def scan(out, d0, d1, initial, op0=ALU.mult, op1=ALU.add):
        rd = [d0, d1]
        a = initial
        if isinstance(initial, V):
            rd.append(initial); a = initial.ap
        cx.op(dve, lambda e: e.tensor_tensor_scan(out=out.ap, data0=d0.ap, data1=d1.ap, initial=a, op0=op0, op1=op1), rd, [out],
              edur(dve, nel(out.ap), False, 2.0))

    def recip(out, in_):
        cx.op(dve, lambda e: e.reciprocal(out=out.ap, in_=in_.ap), [in_], [out], edur(dve, nel(out.ap), is_ps(in_), 4.3))

    def rsqrt_eps(out, in_):
        actf(out, in_, AF.Ln, bias=EPS)
        actf(out, out, AF.Exp, scale=-0.5)

    def copy(eng, out, in_):
        if eng == act:
            cx.op(act, lambda e: e.copy(out=out.ap, in_=in_.ap), [in_], [out], edur(act, nel(out.ap)))
        else:
            cx.op(eng, lambda e: e.tensor_copy(out=out.ap, in_=in_.ap), [in_], [out], edur(eng, nel(out.ap), is_ps(in_)))

    def memset(eng, out, val):
        cx.op(eng, lambda e: e.memset(out.ap, float(val)), [], [out], edur(eng, nel(out.ap)) * 0.5)

    def dma(q, out, in_, sembuf=None, track_out=True, group=None, **kw):
        sb = sembuf or out.buf
        nb = out.ap.shape[0] * nel(out.ap) * (2 if out.ap.dtype == BF16 else 4)
        return cx.dma(q, lambda e: e.dma_start(out=out.ap, in_=in_.ap, **kw), [in_], [out] if track_out else [], sb, nb, group)

    def cc_gather(snd, gat):
        return cx.dma(pool, lambda e: e.collective_compute("AllGather", ALU.bypass, replica_groups=RG,
                                                            ins=[snd.apv], outs=[gat.apv]),
                      [snd], [gat], gat, 9 << 20, None, 1)

    def dma_setup(out, in_, **kw):
        cx.dma_setup(lambda e: e.dma_start(out=out.ap, in_=in_.ap, **kw), out)

    psum = Pool([cx.psum(f"ps{i}") for i in range(8)])
    NSLAB = 6
    slabs = Pool([cx.sbuf([128, 4096], BF16, f"slab{i}") for i in range(NSLAB)])
    ftmp = Pool([cx.sbuf([128, TT], F32, f"ft{i}") for i in range(21)])
    tabs = Pool([cx.sbuf([128, 2 * TT], F32, f"tab{i}") for i in range(2)])
    htmp = Pool([cx.sbuf([128, TT], BF16, f"ht{i}") for i in range(30)])

    hS = [[cx.sbuf([128, TT], F32, f"hS{i}_{k}") for k in range(8)] for i in range(2)]
    hbs = [[cx.sbuf([128, TT], BF16, f"hb{i}_{k}") for k in range(8)] for i in range(2)]
    cur = {"hb": hbs[0]}

    ident = cx.sbuf([128, 128], F32, "ident")
    onesf = cx.sbuf([128, 128], F32, "onesf")
    memset(pool, onesf[:], 1.0)
    cx.op(pool, lambda e: e.affine_select(out=ident.t[:], in_=onesf.t[:], pattern=[[-1, 128]], compare_op=ALU.is_equal,
                                          fill=0.0, base=0, channel_multiplier=1), [onesf], [ident])
    ones_d = cx.sbuf([128, 128], BF16, "ones_d")
    memset(pool, ones_d[:], 1.0 / 1024)
    ones_v = cx.sbuf([128, 128], BF16, "ones_v")
    memset(pool, ones_v[:], 1.0 / 128)
    ones_1 = cx.sbuf([128, 128], BF16, "ones_1")
    memset(pool, ones_1[:], 1.0)
    mask4 = cx.sbuf([128, 512], F32, "mask4")
    onesw = cx.sbuf([128, 128], F32, "onesw")
    memset(pool, onesw[:], 1.0)
    for n in range(4):
        cx.op(pool, lambda e, n=n: e.affine_select(out=mask4.t[:, n * 128:(n + 1) * 128], in_=onesw.t[:, 0:128], pattern=[[1, 128]],
                                                   compare_op=ALU.is_ge, fill=0.0, base=0, channel_multiplier=-1), [onesw], [mask4])
    cmask = cx.sbuf([128, 512], F32, "cmask")
    memset(pool, cmask[:], 1.0)
    for n in range(4):
        memset(pool, cmask[:, n * 128:n * 128 + 1], 0.0)
    iota = ftmp.get()
    cx.op(pool, lambda e: e.iota(iota.t[:], pattern=[[1, 512]], base=0, channel_multiplier=0,
                                 allow_small_or_imprecise_dtypes=True), [], [iota])

    def col_load(name, src_ap, shape):
        b = cx.sbuf(shape, F32, name)
        dma_setup(b[:], V(src_ap[0], src_ap[1]), allow_slow_non_contiguous=True)
        return b

    def cols(dt, l, off, n, name):
        ap = dt.apv[l, off:off + n * 128] if l is not None else dt.apv[off:off + n * 128]
        return col_load(name, (dt, ap.rearrange("(j p) -> p j", p=128)), [128, n])

    g_in = cols(ln_in_g, None, 0, 8, "g_in"); b_in_ln = cols(ln_in_b, None, 0, 8, "b_in_ln")
    sel = cx.sbuf([128, 1], F32, "sel")
    dma_setup(sel[:], V(role_d, role_d.apv))
    nsel = cx.sbuf([128, 1], F32, "nsel")
    ts(dve, nsel[:], sel[:], -1.0, 1.0, ALU.mult, ALU.add)
    ts(dve, g_in[:], g_in[:], sel[:, 0:1], None, ALU.mult)
    ts(dve, b_in_ln[:], b_in_ln[:], sel[:, 0:1], None, ALU.mult)
    LP = []
    for l in range(NL):
        p = {}
        p["b_main"] = cols(b_in, l, 0, 20, f"bmain{l}")
        p["b_s5"] = cols(b_in, l, 2576, 4, f"bs5{l}")
        p["b_gate"] = cols(b_in, l, 3088, 24, f"bgate{l}")
        p["b_lr"] = col_load(f"blr{l}", (b_in, b_in.apv[l, 2560:2576].rearrange("(p o) -> p o", o=1)), [16, 1])
        p["b_qk"] = col_load(f"bqk{l}", (b_in, b_in.apv[l, 1024:1536].rearrange("(j p) -> p j", p=64)), [64, 8])
        p["bq8"] = cx.sbuf([64, 4], F32, f"bq8{l}")
        ts(pool, p["bq8"][:], p["b_qk"][:, 0:4], 0.125, None, ALU.mult)
        p["b_v"] = cx.sbuf([128, 512], F32, f"bv{l}")
        dma_setup(p["b_v"][:], V(b_in, b_in.apv[l:l + 1, 1536:2048].broadcast_to([128, 512])))
        p["ln1g"] = cols(ln1_g, l, 0, 8, f"ln1g{l}"); p["ln1b"] = cols(ln1_b, l, 0, 8, f"ln1b{l}")
        p["ln2g"] = cols(ln2_g, l, 0, 8, f"ln2g{l}"); p["ln2b"] = cols(ln2_b, l, 0, 8, f"ln2b{l}")
        p["ln3g"] = cols(ln3_g, l, 0, 8, f"ln3g{l}"); p["ln3b"] = cols(ln3_b, l, 0, 8, f"ln3b{l}")
        p["convw"] = col_load(f"convw{l}", (conv_w, conv_w.apv[l].rearrange("k (c p) -> p k c", p=128)), [128, 4, 4])
        p["convb"] = cols(conv_b, l, 0, 4, f"convb{l}")
        p["b_a"] = cols(lru_b_a, l, 0, 4, f"ba{l}"); p["b_i"] = cols(lru_b_i, l, 0, 4, f"bi{l}")
        lam = cols(lru_lam, l, 0, 4, f"lam{l}")
        e1 = cx.sbuf([128, 4], F32, f"lrue{l}")
        actf(e1[:], lam[:], AF.Exp, scale=-1.0)
        actf(e1[:], e1[:], AF.Ln, bias=1.0)
        p["cA"] = cx.sbuf([128, 4], F32, f"cA{l}"); p["cA2"] = cx.sbuf([128, 4], F32, f"cA2{l}")
        ts(pool, p["cA"][:], e1[:], -8.0, None, ALU.mult)
        ts(pool, p["cA2"][:], e1[:], -16.0, None, ALU.mult)
        nb = col_load(f"nblr{l}", (gla_b_lr, gla_b_lr.apv[l].rearrange("(j p) -> p j", p=64)), [64, 4])
        p["nb_lr"] = cx.sbuf([64, 4], F32, f"nblr2{l}")
        ts(pool, p["nb_lr"][:], nb[:], -1.0, None, ALU.mult)
        p["w_lr"] = cx.sbuf([16, 256], BF16, f"wlr{l}")
        dma(pool, p["w_lr"][:], V(gla_w_lr, gla_w_lr.apv[l]))
        p["gn"] = col_load(f"gn{l}", (gla_g, gla_g.apv[l].rearrange("(p o) -> p o", o=1)), [128, 1])
        p["s5d"] = cols(s5_d, l, 0, 4, f"s5d{l}"); p["bglu"] = cols(s5_bglu, l, 0, 4, f"bglu{l}")
        lre = cx.sbuf([128, 16], F32, f"lre{l}"); lim = cx.sbuf([128, 16], F32, f"lim{l}"); ldt = cx.sbuf([128, 16], F32, f"ldt{l}")
        for two in range(2):
            dma_setup(lre[two * 64:(two + 1) * 64, :], V(s5_lre, s5_lre.apv[l].rearrange("(j two) p -> two p j", two=2)[two]),
                allow_slow_non_contiguous=True)
            dma_setup(lim[two * 64:(two + 1) * 64, :], V(s5_lim, s5_lim.apv[l].rearrange("(j two) p -> two p j", two=2)[two]),
                allow_slow_non_contiguous=True)
            dma_setup(ldt[two * 64:(two + 1) * 64, :],
                V(s5_ldt, s5_ldt.apv[l].rearrange("(j two) -> two j", two=2)[two:two + 1, :].broadcast_to([64, 16])),
                allow_slow_non_contiguous=True)

        p["lre"] = lre; p["lim"] = lim; p["ldt"] = ldt
        LP.append(p)

    cast_toks = {}

    def cast(dst, src, l, rows, cols_, a=1):
        if a == 1:
            s_ap = src.apv[l]; d_ap = dst.apv[l]
            if len(s_ap.shape) > 2:
                s_ap = s_ap.flatten_outer_dims()
        else:
            s_ap = src.apv[l].rearrange("k (a c) -> (k a) c", a=a)
            d_ap = dst.apv[l].rearrange("k (a c) -> (k a) c", a=a)
        dma(pool, V(dst, d_ap), V(src, s_ap), sembuf=dst)

    for l in range(NL):
        cast(w_in_s, w_in, l, D, D_IN, a=5)
        cast(wa_s, lru_w_a, l, 512, 128); cast(wi_s, lru_w_i, l, 512, 128)
        cast(wglu_s, s5_wglu, l, 512, 512)
        cast(wbr_s, w_branch, l, 1536, D); cast(wmix_s, w_mix, l, D, D)
        cast(wq_s, xa_wq, l, D, D); cast(wo_s, xa_wo, l, D, D)
        cast(wgu_s, w_gu, l, D, 2 * DFF, a=4); cast(wdn_s, w_down, l, DFF, D)

    stg = cx.sbuf([128, 128], BF16, "s5stg0"), cx.sbuf([128, 128], BF16, "s5stg1")
    nstg = [0]
    EBr = cx.sbuf([128, 128], F32, "EBr"); EBi = cx.sbuf([128, 128], F32, "EBi")
    ECr = cx.sbuf([128, 128], F32, "ECr"); ECi = cx.sbuf([128, 128], F32, "ECi")
    Et = cx.sbuf([128, 128], F32, "Et"); Eo = cx.sbuf([128, 128], F32, "Eo")
    for l in range(NL):
        p = LP[l]
        lre = p["lre"]; lim = p["lim"]; ldt = p["ldt"]

        def t16(nm):
            return cx.sbuf([128, 16], F32, f"{nm}{l}")

        dt_ = t16("dt"); actf(dt_[:], ldt[:], AF.Exp)
        th = t16("th"); tt(dve, th[:], lim[:], dt_[:], ALU.mult)
        lrd = t16("lrd"); tt(dve, lrd[:], lre[:], dt_[:], ALU.mult)
        rho = t16("rho"); actf(rho[:], lrd[:], AF.Exp)
        cu = t16("cu"); ts(dve, cu[:], th[:], 1.0 / (2 * PI), None, ALU.mult)
        ck = t16("ck"); ts(dve, ck[:], cu[:], MAGIC, None, ALU.add)
        ts(dve, ck[:], ck[:], -MAGIC, None, ALU.add)
        cfr = t16("cfr"); tt(dve, cfr[:], cu[:], ck[:], ALU.subtract)
        snt = t16("snt"); actf(snt[:], cfr[:], AF.Sin, scale=TWO_PI_S)
        ac = t16("ac"); actf(ac[:], cfr[:], AF.Abs)
        cst = t16("cst"); actf(cst[:], ac[:], AF.Sin, bias=PI / 2, scale=-TWO_PI_S)
        abr = t16("abr"); tt(dve, abr[:], rho[:], cst[:], ALU.mult)
        abi = t16("abi"); tt(dve, abi[:], rho[:], snt[:], ALU.mult)
        abm = t16("abm"); ts(dve, abm[:], abr[:], -1.0, None, ALU.add)
        den = t16("den"); tt(dve, den[:], lre[:], lre[:], ALU.mult)
        t2 = t16("t2"); tt(dve, t2[:], lim[:], lim[:], ALU.mult)
        tt(dve, den[:], den[:], t2[:], ALU.add)
        rden = t16("rden"); recip(rden[:], den[:])
        fre = t16("fre"); fim = t16("fim"); t3 = t16("t3")
        tt(dve, fre[:], abm[:], lre[:], ALU.mult); tt(dve, t3[:], abi[:], lim[:], ALU.mult)
        tt(dve, fre[:], fre[:], t3[:], ALU.add); tt(dve, fre[:], fre[:], rden[:], ALU.mult)
        tt(dve, fim[:], abi[:], lre[:], ALU.mult); tt(dve, t3[:], abm[:], lim[:], ALU.mult)
        tt(dve, fim[:], fim[:], t3[:], ALU.subtract); tt(dve, fim[:], fim[:], rden[:], ALU.mult)
        nsnt = t16("nsnt"); ts(dve, nsnt[:], snt[:], -1.0, None, ALU.mult)
        uT = t16("uT"); ts(dve, uT[:], cfr[:], float(TT), None, ALU.mult)
        kT = t16("kT"); ts(dve, kT[:], uT[:], MAGIC, None, ALU.add)
        ts(dve, kT[:], kT[:], -MAGIC, None, ALU.add)
        fT = t16("fT"); tt(dve, fT[:], uT[:], kT[:], ALU.subtract)
        snT = t16("snT"); actf(snT[:], fT[:], AF.Sin, scale=TWO_PI_S)
        aT = t16("aT"); actf(aT[:], fT[:], AF.Abs)
        csT = t16("csT"); actf(csT[:], aT[:], AF.Sin, bias=PI / 2, scale=-TWO_PI_S)
        nsnT = t16("nsnT"); ts(dve, nsnT[:], snT[:], -1.0, None, ALU.mult)
        p.update(th=th, rho=rho, cst=cst, snt=snt, nsnt=nsnt, cfr=cfr, snT=snT, csT=csT, nsnT=nsnT)
        for j in range(16):
            tb = tabs.get(); kf = ftmp.get(); fa = ftmp.get()
            ts(dve, kf[:], iota[:], cfr[:, j:j + 1], MAGIC, ALU.mult, ALU.add)
            actf(kf[:], kf[:], AF.Identity, bias=-MAGIC)
            stt(dve, fa[:], iota[:], cfr[:, j:j + 1], kf[:], ALU.mult, ALU.subtract)
            actf(tb[:, TT:2 * TT], fa[:], AF.Sin, scale=TWO_PI_S)
            actf(fa[:], fa[:], AF.Abs)
            actf(tb[:, 0:TT], fa[:], AF.Sin, bias=PI / 2, scale=-TWO_PI_S)
            dma(sp, V(tab_s, tab_s.apv[l, j]), tb[:], sembuf=tab_s)
            tabs.put(tb); ftmp.put(kf); ftmp.put(fa)

        for j in range(16):
            for E_ in (EBr, EBi, ECr, ECi):
                memset(pool, E_[:], 0.0)
            for two in range(2):
                g = 2 * j + two
                co = (g % 8) * 16
                dma(sp, EBr[two * 64:(two + 1) * 64, co:co + 16], V(s5_bre, s5_bre.apv[l, g]))
                dma(sp, EBi[two * 64:(two + 1) * 64, co:co + 16], V(s5_bim, s5_bim.apv[l, g]))
                dma(sp, ECr[co:co + 16, two * 64:(two + 1) * 64], V(s5_cre, s5_cre.apv[l, g]))
                dma(sp, ECi[co:co + 16, two * 64:(two + 1) * 64], V(s5_cim, s5_cim.apv[l, g]))
            ts(dve, Et[:], EBi[:], fim[:, j:j + 1], None, ALU.mult)
            stt(dve, Eo[:], EBr[:], fre[:, j:j + 1], Et[:], ALU.mult, ALU.subtract)
            def emit(srcE, idx, sc):
                ps = psum.get()
                transpose(ps[:, 0:128], srcE[:], ident[:])
                sg = stg[nstg[0] % 2]; nstg[0] += 1
                actf(sg[:], ps[:, 0:128], AF.Copy, scale=sc)
                psum.put(ps)
                dma(sp, V(s5w_s, s5w_s.apv[l, :, idx * 128:(idx + 1) * 128]), sg[:], sembuf=s5w_s)

            emit(Eo, 0 * 16 + j, 1.0)
            ts(dve, Et[:], EBr[:], fim[:, j:j + 1], None, ALU.mult)
            stt(dve, Eo[:], EBi[:], fre[:, j:j + 1], Et[:], ALU.mult, ALU.add)
            emit(Eo, 1 * 16 + j, 1.0)
            emit(ECr, 2 * 16 + j, 1.0)
            emit(ECi, 3 * 16 + j, -1.0)
            emit(ECr, 4 * 16 + j, -1.0)

    ftmp.put(iota)
    kvsw = cx.sbuf([1, 2], F32, "kvsw")
    memT_b = [ftmp.get(), ftmp.get()]

    def memT(k, lo=0, hi=256):
        bb = memT_b[k // 4]
        return V(bb, bb.t[:].bitcast(BF16).rearrange("p (k m) -> p k m", k=4)[:, k % 4, lo:hi])

    for mc in range(2):
        xh = [ftmp.get(), ftmp.get()]
        for hf in range(2):
            dma(sp, xh[hf][:], V(mem_d, mem_d.apv[mc * 128:(mc + 1) * 128, hf * 512:(hf + 1) * 512]))
        for k in range(8):
            ps = psum.get()
            transpose(ps[:, 0:128], xh[k // 4][:, (k % 4) * 128:(k % 4 + 1) * 128], ident[:])
            copy(act, memT(k, mc * 128, (mc + 1) * 128), ps[:, 0:128])
            psum.put(ps)
        ftmp.put(xh[0]); ftmp.put(xh[1])
    for l in range(NL):
        for q4 in range(4):
            sl = slabs.get()
            slv = sl.t[:].rearrange("p (k c) -> p k c", k=8)
            cx.dma(pool, lambda e, slv=slv, q4=q4, l=l: e.dma_start(
                       out=slv, in_=xa_wkv.apv[l].rearrange("(k p) c -> p k c", p=128)[:, :, q4 * 512:(q4 + 1) * 512]),
                   [xa_wkv], [sl, kvsw], kvsw, 2 << 20)
            if q4 < 2:
                stgk = htmp.get()
                for cc in range(4):
                    ps = psum.get()
                    for k in range(8):
                        mm(ps[:, 0:256], V(sl, slv[:, k, cc * 128:(cc + 1) * 128]), memT(k), k == 0, k == 7)
                    copy(act, stgk[:, (cc % 2) * 256:(cc % 2 + 1) * 256], ps[:, 0:256])
                    psum.put(ps)
                    if cc % 2 == 1:
                        c0 = q4 * 4 + cc - 1
                        dma(sp, V(kt_s, kt_s.apv[l, :, c0 * 256:(c0 + 2) * 256]), stgk[:], sembuf=kt_s)
                        if cc == 1:
                            htmp.put(stgk); stgk = htmp.get()
                htmp.put(stgk)
            else:
                for mc in range(2):
                    ps = psum.get()
                    for k in range(8):
                        mm(ps[:], memT(k, mc * 128, (mc + 1) * 128), V(sl, slv[:, k, :]), k == 0, k == 7)
                    stgv = htmp.get()
                    copy(act, stgv[:], ps[:])
                    psum.put(ps)
                    dma(sp, V(v_s, v_s.apv[l, :, mc * 1024 + (q4 - 2) * 512: mc * 1024 + (q4 - 1) * 512]), stgv[:], sembuf=v_s)
                    htmp.put(stgv)
            slabs.put(sl)

    ftmp.put(memT_b[0]); ftmp.put(memT_b[1])
    ST = []
    for l in range(NL):
        s = {}
        s["ubuf"] = [cx.sbuf([128, TT + 3], F32, f"ubuf{l}_{c}") for c in range(4)]
        s["hl"] = cx.sbuf([128, 4], F32, f"hlst{l}")
        memset(pool, s["hl"][:], 0.0)
        for c in range(4):
            memset(pool, s["ubuf"][c][:, TT:TT + 3], 0.0)
        s["S"] = [cx.sbuf([64, 128], F32, f"S{l}_{h}") for h in range(4)]
        s["Sb"] = [cx.sbuf([64, 128], BF16, f"Sb{l}_{h}") for h in range(4)]
        for h in range(4):
            memset(pool, s["S"][h][:], 0.0); memset(pool, s["Sb"][h][:], 0.0)
        s["zr"] = cx.sbuf([128, 16], F32, f"zr{l}"); s["zi"] = cx.sbuf([128, 16], F32, f"zi{l}")
        memset(pool, s["zr"][:], 0.0); memset(pool, s["zi"][:], 0.0)
        ST.append(s)

    def load_slab(src, src_ap, shape_str=None, **kw):
        sl = slabs.get()
        n = 1
        for d_ in src_ap.shape[1:]:
            n *= d_
        assert n <= 4096, n
        if len(src_ap.shape) == 3:
            view = sl.t[:, 0:n].rearrange("p (k c) -> p k c", k=src_ap.shape[1])
        else:
            view = sl.t[:, 0:n]
        dma(sp, V(sl, view), V(src, src_ap))
        return sl, view

    def kview(dt, l, c0, w, rows=None):
        ap = dt.apv[l] if rows is None else dt.apv[l, rows[0]:rows[1]]
        return ap.rearrange("(k p) c -> p k c", p=128)[:, :, c0:c0 + w]

    def layer_norm(r, hbo, gcol, bcol):
        cx.tag = "ln"
        rb = [htmp.get() for _ in range(8)]
        for k in range(8):
            copy(act, rb[k][:], r[k][:])
        mean = psum.get()
        for k in range(8):
            mm(mean[:], ones_d[:], rb[k][:], k == 0, k == 7)
        for k in range(8):
            tt(dve, r[k][:], r[k][:], mean[:], ALU.subtract)
        psum.put(mean)
        for k in range(8):
            actf(rb[k][:], r[k][:], AF.Square)
        var = psum.get()
        for k in range(8):
            mm(var[:], ones_d[:], rb[k][:], k == 0, k == 7)
        rstd = ftmp.get()
        rsqrt_eps(rstd[:], var[:])
        psum.put(var)
        for k in range(8):
            htmp.put(rb[k])
        for k in range(8):
            tt(dve, r[k][:], r[k][:], rstd[:], ALU.mult)
            actf(hbo[k][:], r[k][:], AF.Identity, bias=bcol[:, k:k + 1], scale=gcol[:, k:k + 1])
            actf(r[k][:], r[k][:], AF.Identity, bias=bcol[:, k:k + 1], scale=gcol[:, k:k + 1])
        ftmp.put(rstd)

    def proj_fm(ps, slab, view, col0, m, nk=8, rhs=None):
        rhs = rhs or cur["hb"]
        for k in range(nk):
            mm(ps[0:m, :], V(slab, view[:, k, col0:col0 + m]), rhs[k][:], k == 0, k == nk - 1)

    def mixer(l, hcur, rout, first_tile):
        p = LP[l]; s = ST[l]
        ya = [htmp.get() for _ in range(4)]

        def genA():
            cx.tag = "mixA"
            slu, vu = load_slab(w_in_s, kview(w_in_s, l, 0, 512))
            slg, vg = load_slab(w_in_s, kview(w_in_s, l, 512, 512))
            slw, vw = load_slab(wa_s, wa_s.apv[l].rearrange("(h k) m -> k h m", k=128))
            slw2, vw2 = load_slab(wi_s, wi_s.apv[l].rearrange("(h k) m -> k h m", k=128))
            for c in range(4):
                ub = s["ubuf"][c]
                copy(dve, ub[:, 0:3], ub[:, TT:TT + 3])
                ps = psum.get()
                proj_fm(ps, slu, vu, c * 128, 128)
                actf(ub[:, 3:TT + 3], ps[:], AF.Identity, bias=p["b_main"][:, c:c + 1])
                psum.put(ps)
                uc = ftmp.get()
                ts(dve, uc[:], ub[:, 0:TT], p["convw"][:, 0, c:c + 1], p["convb"][:, c:c + 1], ALU.mult, ALU.add)
                for k in range(1, 4):
                    stt(dve if k % 2 else pool, uc[:], ub[:, k:k + TT], p["convw"][:, k, c:c + 1], uc[:], ALU.mult, ALU.add)
                ucb = htmp.get()
                copy(act, ucb[:], uc[:])
                psr = psum.get(); psi = psum.get()
                mm(psr[:], V(slw, vw[:, c, :]), ucb[:], True, True)
                mm(psi[:], V(slw2, vw2[:, c, :]), ucb[:], True, True)
                htmp.put(ucb)
                rg = ftmp.get(); ig = ftmp.get()
                actf(rg[:], psr[:], AF.Sigmoid, bias=p["b_a"][:, c:c + 1])
                actf(ig[:], psi[:], AF.Sigmoid, bias=p["b_i"][:, c:c + 1])
                psum.put(psr); psum.put(psi)
                a_ = ftmp.get(); m_ = ftmp.get()
                actf(a_[:], rg[:], AF.Exp, scale=p["cA"][:, c:c + 1])
                actf(m_[:], rg[:], AF.Exp, scale=p["cA2"][:, c:c + 1])
                actf(m_[:], m_[:], AF.Sqrt, bias=1.0, scale=-1.0)
                tt(pool, ig[:], ig[:], uc[:], ALU.mult)
                tt(dve, ig[:], ig[:], m_[:], ALU.mult)
                scan(rg[:], a_[:], ig[:], s["hl"][:, c:c + 1])
                copy(dve, s["hl"][:, c:c + 1], rg[:, TT - 1:TT])
                psg = psum.get()
                proj_fm(psg, slg, vg, c * 128, 128)
                actf(m_[:], psg[:], AF.Gelu, bias=p["b_main"][:, 4 + c:5 + c])
                psum.put(psg)
                tt(pool, ya[c][:], rg[:], m_[:], ALU.mult)
                for b_ in (uc, rg, ig, a_, m_):
                    ftmp.put(b_)
                cx.tag = "mixC"
                yield
                cx.tag = "mixA"
            for s_ in (slu, slg, slw, slw2):
                slabs.put(s_)

            cx.tag = "mixC"

        yb = [htmp.get() for _ in range(4)]

        def genB():
            cx.tag = "mixB"
            slq, vq = load_slab(w_in_s, kview(w_in_s, l, 1024, 512))
            slv_, vv = load_slab(w_in_s, kview(w_in_s, l, 1536, 512))
            slo, vo = load_slab(w_in_s, kview(w_in_s, l, 2048, 512))
            sllr, vlr = load_slab(w_in_s, kview(w_in_s, l, 2560, 16))
            ps = psum.get()
            proj_fm(ps, sllr, vlr, 0, 16)
            lrb = htmp.get()
            actf(lrb[0:16, :], ps[0:16, :], AF.Identity, bias=p["b_lr"][:, 0:1])
            psum.put(ps); slabs.put(sllr)
            vtok = [htmp.get() for _ in range(4)]
            for n in range(4):
                ps = psum.get()
                for k in range(8):
                    mm(ps[:], cur["hb"][k][:, n * 128:(n + 1) * 128], V(slv_, vv[:, k, :]), k == 0, k == 7)
                tt(dve, vtok[n][:], ps[:], p["b_v"][:], ALU.add)
                psum.put(ps)
            slabs.put(slv_)
            cx.tag = "mixC"
            yield
            cx.tag = "mixB"
            for h in range(4):
                psq = psum.get(); psk = psum.get(); psx = psum.get()
                proj_fm(psq, slq, vq, h * 64, 64)
                proj_fm(psk, slq, vq, 256 + h * 64, 64)
                mm(psx[0:64, :], p["w_lr"][:, h * 64:(h + 1) * 64], lrb[0:16, :], True, True)
                qf = ftmp.get(); kf = ftmp.get(); sp_ = ftmp.get()
                actf(qf[0:64, :], psq[0:64, :], AF.Identity, bias=p["bq8"][:, h:h + 1], scale=0.125)
                actf(kf[0:64, :], psk[0:64, :], AF.Identity, bias=p["b_qk"][:, 4 + h:5 + h])
                actf(sp_[0:64, :], psx[0:64, :], AF.Exp, bias=p["nb_lr"][:, h:h + 1], scale=-1.0)
                actf(sp_[0:64, :], sp_[0:64, :], AF.Ln, bias=1.0)
                psum.put(psq); psum.put(psk); psum.put(psx)
                gn_ = ftmp.get()
                scan(gn_[0:64, :], cmask[0:64, :], sp_[0:64, :], 0.0)
                eg = ftmp.get()
                actf(eg[0:64, :], gn_[0:64, :], AF.Exp, scale=-1.0 / 16)
                actf(sp_[0:64, :], gn_[0:64, :], AF.Exp, scale=1.0 / 16)
                qd = htmp.get(); kib = htmp.get()
                tt(dve, qd[0:64, :], qf[0:64, :], eg[0:64, :], ALU.mult)
                tt(pool, kf[0:64, :], kf[0:64, :], sp_[0:64, :], ALU.mult)
                copy(act, kib[0:64, :], kf[0:64, :])
                for n in range(4):
                    ts(dve, gn_[0:64, n * 128:(n + 1) * 128], kf[0:64, n * 128:(n + 1) * 128],
                       eg[0:64, n * 128 + 127:n * 128 + 128], None, ALU.mult)
                psa = psum.get()
                for n in range(4):
                    mm(psa[:, n * 128:(n + 1) * 128], kib[0:64, n * 128:(n + 1) * 128], qd[0:64, n * 128:(n + 1) * 128], True, True)
                attb = htmp.get()
                tt(dve, attb[:], psa[:], mask4[:], ALU.mult)
                psum.put(psa)
                kend = htmp.get()
                pst = psum.get()
                for n in range(4):
                    transpose(pst[:, n * 64:(n + 1) * 64], gn_[0:64, n * 128:(n + 1) * 128], ident[0:64, 0:64])
                copy(act, kend[:, 0:256], pst[:, 0:256])
                psum.put(pst)
                pso = psum.get()
                S = s["S"][h]; Sb = s["Sb"][h]
                for n in range(4):
                    mm(pso[:, n * 128:(n + 1) * 128], vtok[n][:, h * 128:(h + 1) * 128], attb[:, n * 128:(n + 1) * 128], True, False)
                    mm(pso[:, n * 128:(n + 1) * 128], Sb[:], qd[0:64, n * 128:(n + 1) * 128], False, True)
                    psd = psum.get()
                    mm(psd[0:64, 0:128], kend[:, n * 64:(n + 1) * 64], vtok[n][:, h * 128:(h + 1) * 128], True, True)
                    stt(dve, S[:], S[:], eg[0:64, n * 128 + 127:n * 128 + 128], psd[0:64, 0:128], ALU.mult, ALU.add)
                    psum.put(psd)
                    copy(pool, Sb[:], S[:])
                for b_ in (qd, kib, attb, kend):
                    htmp.put(b_)
                for b_ in (qf, kf, sp_, gn_, eg):
                    ftmp.put(b_)
                of = ftmp.get(); sqb = htmp.get()
                actf(of[:], pso[:], AF.Identity, scale=p["gn"][:, 0:1])
                actf(sqb[:], pso[:], AF.Square)
                psum.put(pso)
                psm = psum.get()
                mm(psm[:], ones_v[:], sqb[:], True, True)
                htmp.put(sqb)
                rs_ = ftmp.get()
                rsqrt_eps(rs_[:], psm[:])
                psum.put(psm)
                tt(pool, of[:], of[:], rs_[:], ALU.mult)
                psg = psum.get()
                proj_fm(psg, slo, vo, h * 128, 128)
                actf(rs_[:], psg[:], AF.Silu, bias=p["b_main"][:, 16 + h:17 + h])
                psum.put(psg)
                tt(dve, yb[h][:], of[:], rs_[:], ALU.mult)
                ftmp.put(of); ftmp.put(rs_)
                cx.tag = "mixC"
                yield
                cx.tag = "mixB"
            for b_ in vtok:
                htmp.put(b_)
            htmp.put(lrb)
            slabs.put(slq); slabs.put(slo)

            cx.tag = "mixC"

        ys = [ya, yb, None]
        mg = []
        acc = []

        def merge_steps(n):
            tag0 = cx.tag
            cx.tag = "merge"
            if n == 0:
                acc.extend(ftmp.get() for _ in range(8))
            if n == 2:
                mg.extend(htmp.get() for _ in range(8))
            wsl, wv = load_slab(wbr_s, wbr_s.apv[l, n * 512:(n + 1) * 512].rearrange("(k p) c -> p k c", p=128))
            gsl = [None, None]
            cx.tag = tag0
            for k in range(8):
                tag0 = cx.tag
                cx.tag = "merge"
                if k % 4 == 0:
                    gsl[k // 4] = load_slab(w_in_s, kview(w_in_s, l, 3088 + n * 1024 + (k // 4) * 512, 512))
                psg = psum.get()
                proj_fm(psg, gsl[k // 4][0], gsl[k // 4][1], (k % 4) * 128, 128)
                gt = ftmp.get()
                actf(gt[:], psg[:], AF.Sigmoid, bias=p["b_gate"][:, n * 8 + k:n * 8 + k + 1])
                psum.put(psg)
                psb = psum.get()
                proj_fm(psb, wsl, wv, k * 128, 128, nk=4, rhs=ys[n])
                if n == 0:
                    tt(dve, acc[k][:], psb[:], gt[:], ALU.mult)
                else:
                    tt(dve, gt[:], psb[:], gt[:], ALU.mult)
                    if n == 1:
                        tt(pool, acc[k][:], acc[k][:], gt[:], ALU.add)
                    else:
                        tt(pool, mg[k][:], acc[k][:], gt[:], ALU.add)
                psum.put(psb)
                ftmp.put(gt)
                if k % 4 == 3:
                    slabs.put(gsl[k // 4][0])
                if k == 7:
                    slabs.put(wsl)
                cx.tag = tag0
                yield

        def others():
            yield from genA()
            gb_ = genB()
            m0 = merge_steps(0)
            for _ in gb_:
                yield
                for _k in range(2):
                    if next(m0, "end") != "end":
                        yield
            for _ in m0:
                yield
            yield from merge_steps(1)

        mgen = others()

        cx.tag = "mixC"
        yc = [htmp.get() for _ in range(4)]
        ys[2] = yc
        slu5, vu5 = load_slab(w_in_s, kview(w_in_s, l, 2576, 512))
        u5 = [ftmp.get() for _ in range(4)]; u5b = [htmp.get() for _ in range(4)]
        for c in range(4):
            ps = psum.get()
            proj_fm(ps, slu5, vu5, c * 128, 128)
            actf(u5[c][:], ps[:], AF.Identity, bias=p["b_s5"][:, c:c + 1])
            psum.put(ps)
            copy(pool, u5b[c][:], u5[c][:])
        slabs.put(slu5)
        s5v = s5w_s.apv[l].rearrange("p (kind j m) -> p kind j m", kind=5, j=16)
        for c in range(4):
            slBC = slabs.get()
            vBC = slBC.t[:, 0:2560].rearrange("p (kind j m) -> p kind j m", kind=5, j=4)
            dma(sp, V(slBC, vBC), V(s5w_s, s5v[:, :, 4 * c:4 * c + 4, :]))
            psy = psum.get()
            for jj in range(4):
                j = c * 4 + jj
                tb = tabs.get()
                dma(sp, tb[:], V(tab_s, tab_s.apv[l, j]))
                cs = tb[:, 0:TT]; sn = tb[:, TT:2 * TT]
                pbr = psum.get(); pbi = psum.get()
                mm(pbr[:], V(slBC, vBC[:, 0, jj, :]), u5b[c][:], True, True)
                mm(pbi[:], V(slBC, vBC[:, 1, jj, :]), u5b[c][:], True, True)
                br = ftmp.get(); bi = ftmp.get(); t1 = ftmp.get(); t2 = ftmp.get()
                copy(act, br[:], pbr[:]); copy(act, bi[:], pbi[:])
                psum.put(pbr); psum.put(pbi)
                tt(dve, t1[:], br[:], sn, ALU.mult)
                tt(pool, br[:], br[:], cs, ALU.mult)
                tt(dve, t2[:], bi[:], sn, ALU.mult)
                tt(pool, bi[:], bi[:], cs, ALU.mult)
                tt(pool, br[:], br[:], t2[:], ALU.add)
                tt(dve, bi[:], bi[:], t1[:], ALU.subtract)
                rho_b = V(p["rho"], p["rho"].t[:, j:j + 1].broadcast_to([128, TT]))
                scan(br[:], rho_b, br[:], s["zr"][:, j:j + 1])
                scan(bi[:], rho_b, bi[:], s["zi"][:, j:j + 1])
                actf(t1[:, 0:1], bi[:, TT - 1:TT], AF.Identity, scale=p["nsnT"][:, j:j + 1])
                actf(t1[:, 1:2], bi[:, TT - 1:TT], AF.Identity, scale=p["csT"][:, j:j + 1])
                actf(s["zr"][:, j:j + 1], br[:, TT - 1:TT], AF.Identity, scale=p["csT"][:, j:j + 1], bias=t1[:, 0:1])
                actf(s["zi"][:, j:j + 1], br[:, TT - 1:TT], AF.Identity, scale=p["snT"][:, j:j + 1], bias=t1[:, 1:2])
                p1 = htmp.get(); p2 = htmp.get(); p3 = htmp.get(); p4 = htmp.get()
                tt(pool, p1[:], br[:], cs, ALU.mult)
                tt(dve, p2[:], bi[:], sn, ALU.mult)
                tt(pool, p3[:], br[:], sn, ALU.mult)
                tt(dve, p4[:], bi[:], cs, ALU.mult)
                tabs.put(tb)
                mm(psy[:], V(slBC, vBC[:, 2, jj, :]), p1[:], jj == 0, False)
                mm(psy[:], V(slBC, vBC[:, 4, jj, :]), p2[:], False, False)
                mm(psy[:], V(slBC, vBC[:, 3, jj, :]), p3[:], False, False)
                mm(psy[:], V(slBC, vBC[:, 3, jj, :]), p4[:], False, jj == 3)
                for b_ in (br, bi, t1, t2):
                    ftmp.put(b_)
                for b_ in (p1, p2, p3, p4):
                    htmp.put(b_)
                next(mgen, None)
                if j < 10:
                    next(mgen, None)
            slabs.put(slBC)
            stt(dve, u5[c][:], u5[c][:], p["s5d"][:, c:c + 1], psy[:], ALU.mult, ALU.add)
            psum.put(psy)
            actf(u5[c][:], u5[c][:], AF.Gelu)
            copy(pool, u5b[c][:], u5[c][:])
        for _ in mgen:
            pass
        slG, vG = load_slab(wglu_s, kview(wglu_s, l, 0, 512))
        for c in range(4):
            ps = psum.get()
            proj_fm(ps, slG, vG, c * 128, 128, nk=4, rhs=u5b)
            sg_ = ftmp.get()
            actf(sg_[:], ps[:], AF.Sigmoid, bias=p["bglu"][:, c:c + 1])
            psum.put(ps)
            tt(dve, yc[c][:], u5[c][:], sg_[:], ALU.mult)
            ftmp.put(sg_)
        for c in range(4):
            ftmp.put(u5[c]); htmp.put(u5b[c])
        slabs.put(slG)
        for _ in merge_steps(2):
            pass
        cx.tag = "merge"
        for k in range(8):
            ftmp.put(acc[k])
        for b_ in ya + yb + yc:
            htmp.put(b_)
        for half in range(2):
            slm, vm = load_slab(wmix_s, kview(wmix_s, l, half * 512, 512))
            for kk in range(4):
                k = half * 4 + kk
                ps = psum.get()
                proj_fm(ps, slm, vm, kk * 128, 128, rhs=mg)
                stt(dve, rout[k][:], hcur[k][:], ALPHA, ps[:], ALU.mult, ALU.add)
                psum.put(ps)
            slabs.put(slm)
        for k in range(8):
            htmp.put(mg[k])

    def xattn(l, hcur, rout):
        cx.tag = "xattn"
        qT = [htmp.get() for _ in range(8)]
        for half in range(2):
            slq, vq = load_slab(wq_s, kview(wq_s, l, half * 512, 512))
            for kk in range(4):
                ps = psum.get()
                proj_fm(ps, slq, vq, kk * 128, 128)
                actf(qT[half * 4 + kk][:], ps[:], AF.Copy, scale=1.0 / 16)
                psum.put(ps)
            slabs.put(slq)
        slk, vk = load_slab(kt_s, kt_s.apv[l].rearrange("p (c m) -> p c m", m=256))
        slvv, vvv = load_slab(v_s, v_s.apv[l].rearrange("p (c m) -> p c m", m=1024))
        ob = [htmp.get() for _ in range(8)]
        for h in range(4):
            pT = [htmp.get(), htmp.get()]
            for mc in range(2):
                ps = psum.get()
                for hc in range(2):
                    mm(ps[:], V(slk, vk[:, h * 2 + hc, mc * 128:(mc + 1) * 128]), qT[h * 2 + hc][:], hc == 0, hc == 1)
                actf(pT[mc][:], ps[:], AF.Exp)
                psum.put(ps)
            psd = psum.get()
            for mc in range(2):
                mm(psd[:], ones_1[:], pT[mc][:], mc == 0, mc == 1)
            rd = ftmp.get()
            actf(rd[:], psd[:], AF.Ln)
            actf(rd[:], rd[:], AF.Exp, scale=-1.0)
            psum.put(psd)
            for hc in range(2):
                ps = psum.get()
                for mc in range(2):
                    mm(ps[:], V(slvv, vvv[:, mc, h * 256 + hc * 128:h * 256 + (hc + 1) * 128]), pT[mc][:], mc == 0, mc == 1)
                tt(dve, ob[h * 2 + hc][:], ps[:], rd[:], ALU.mult)
                psum.put(ps)
            ftmp.put(rd); htmp.put(pT[0]); htmp.put(pT[1])
        slabs.put(slk); slabs.put(slvv)
        for b_ in qT:
            htmp.put(b_)
        for half in range(2):
            slo, vo = load_slab(wo_s, kview(wo_s, l, half * 512, 512))
            for kk in range(4):
                k = half * 4 + kk
                ps = psum.get()
                proj_fm(ps, slo, vo, kk * 128, 128, rhs=ob)
                stt(dve, rout[k][:], hcur[k][:], ALPHA, ps[:], ALU.mult, ALU.add)
                psum.put(ps)
            slabs.put(slo)
        for b_ in ob:
            htmp.put(b_)

    def ffn(l, hcur, rout):
        cx.tag = "ffn"
        ab = [htmp.get() for _ in range(22)]
        for q in range(6):
            w = 512 if q < 5 else 256
            slg, vg = load_slab(wgu_s, kview(wgu_s, l, q * 512, w))
            slu, vu = load_slab(wgu_s, kview(wgu_s, l, DFF + q * 512, w))
            for kk in range(w // 128):
                c = q * 4 + kk
                psg = psum.get(); psu = psum.get()
                proj_fm(psg, slg, vg, kk * 128, 128)
                proj_fm(psu, slu, vu, kk * 128, 128)
                sg_ = ftmp.get()
                actf(sg_[:], psg[:], AF.Silu)
                tt(dve, ab[c][:], psu[:], sg_[:], ALU.mult)
                psum.put(psg); psum.put(psu); ftmp.put(sg_)
            slabs.put(slg); slabs.put(slu)
        for k in range(8):
            sld, vd = load_slab(wdn_s, kview(wdn_s, l, k * 128, 128))
            ps = psum.get()
            proj_fm(ps, sld, vd, 0, 128, nk=22, rhs=ab)
            stt(dve, rout[k][:], hcur[k][:], ALPHA, ps[:], ALU.mult, ALU.add)
            psum.put(ps)
            slabs.put(sld)
        for b_ in ab:
            htmp.put(b_)

    out_toks = []
    ostg = [cx.sbuf([128, 512], F32, f"ostg{i}") for i in range(2)]
    nx = [0]; no = [0]
    for it in range(n_tiles + LAG):
        hcur = hS[it % 2]; cur["hb"] = hbs[it % 2]
        cx.itn = it
        cx.tag = "entry"
        if it < n_tiles:
            for n in range(4):
                r0 = it * TT + n * 128
                xh = [ftmp.get(), ftmp.get()]
                st6 = ftmp.get()
                for hf in range(2):
                    dma(sp, xh[hf][:], V(x_d, x_d.apv[r0:r0 + 128, hf * 512:(hf + 1) * 512]))
                    cx.op(dve, lambda e, hf=hf, xb=xh[hf], st6=st6: e.bn_stats(out=st6.t[:, hf * 6:(hf + 1) * 6], in_=xb.t[:]), [xh[hf]], [st6], 0.65)
                cx.op(dve, lambda e, st6=st6: e.bn_aggr(out=st6.t[:, 16:18], in_=st6.t[:, 0:12]), [st6], [st6], 0.2)
                rsqrt_eps(st6[:, 18:19], st6[:, 17:18])
                for hf in range(2):
                    ts(dve, xh[hf][:], xh[hf][:], st6[:, 16:17], st6[:, 18:19], ALU.subtract, ALU.mult)
                ftmp.put(st6)
                for k in range(8):
                    ps = psum.get()
                    transpose(ps[:, 0:128], xh[k // 4][:, (k % 4) * 128:(k % 4 + 1) * 128], ident[:])
                    actf(hcur[k][:, n * 128:(n + 1) * 128], ps[:, 0:128], AF.Identity, bias=b_in_ln[:, k:k + 1], scale=g_in[:, k:k + 1])
                    psum.put(ps)
                ftmp.put(xh[0]); ftmp.put(xh[1])
        if it >= LAG:
            cx.tag = "recv"
            gb = gat_d[(it - LAG) % 2]
            if LAG == 1:
                cc_gather(snd_d[(it - 1) % 3], gb)
            for k in range(8):
                rk = ftmp.get()
                dma(sp, rk[:], V(gb, gb.apv[0:128, k * TT:(k + 1) * TT]))
                if it < n_tiles:
                    stt(dve, hcur[k][:], rk[:], nsel[:, 0:1], hcur[k][:], ALU.mult, ALU.add)
                else:
                    ts(dve, hcur[k][:], rk[:], nsel[:, 0:1], None, ALU.mult)
                ftmp.put(rk)
        for k in range(8):
            copy(act, cur["hb"][k][:], hcur[k][:])
        stage = 0
        for l in range(NL):
            p = LP[l]
            for (fn, g_, b_) in ((lambda a, b: mixer(l, a, b, it == 0), p["ln1g"], p["ln1b"]),
                                 (lambda a, b: xattn(l, a, b), p["ln2g"], p["ln2b"]),
                                 (lambda a, b: ffn(l, a, b), p["ln3g"], p["ln3b"])):
                stage += 1
                if stage > n_stage - 1:
                    continue
                fn(hcur, hcur)
                layer_norm(hcur, cur["hb"], g_, b_)
        cx.tag = "out"
        if it < n_tiles:
            sb_ = snd_d[it % 3]
            grp = {}
            for k in range(8):
                dma(sp, V(sb_, sb_.apv[:, k * TT:(k + 1) * TT]), hcur[k][:], sembuf=sb_, group=grp)
            if LAG > 1:
                cc_gather(sb_, gat_d[it % 2])
        if it >= LAG:
            for n in range(4):
                for hf in range(2):
                    ob_ = ostg[no[0] % 2]; no[0] += 1
                    for kk in range(4):
                        k = hf * 4 + kk
                        ps = psum.get()
                        transpose(ps[:, 0:128], hcur[k][:, n * 128:(n + 1) * 128], ident[:])
                        copy(act if k % 2 else dve, ob_[:, kk * 128:(kk + 1) * 128], ps[:, 0:128])
                        psum.put(ps)
                    r0 = (it - LAG) * TT + n * 128
                    out_toks.append(dma(sp, V(out_d, out_d.apv[r0:r0 + 128, hf * 512:(hf + 1) * 512]), ob_[:], sembuf=ob_, track_out=False))
        if it == LAG - 1:
            cx.tag = "reset"
            for l in range(NL):
                s_ = ST[l]
                for c in range(4):
                    ts(dve, s_["ubuf"][c][:, TT:TT + 3], s_["ubuf"][c][:, TT:TT + 3], sel[:, 0:1], None, ALU.mult)
                ts(dve, s_["hl"][:], s_["hl"][:], sel[:, 0:1], None, ALU.mult)
                for h in range(4):
                    ts(dve, s_["S"][h][:], s_["S"][h][:], sel[0:64, 0:1], None, ALU.mult)
                    ts(dve, s_["Sb"][h][:], s_["Sb"][h][:], sel[0:64, 0:1], None, ALU.mult)
                ts(dve, s_["zr"][:], s_["zr"][:], sel[:, 0:1], None, ALU.mult)
                ts(dve, s_["zi"][:], s_["zi"][:], sel[:, 0:1], None, ALU.mult)
    cx.final_wait(sp, out_toks[-2:])


_WNAMES = ["ln_in_g", "ln_in_b", "w_in", "b_in", "lru_conv_w", "lru_conv_b", "lru_w_a", "lru_b_a", "lru_w_i", "lru_b_i",
           "lru_lambda", "gla_w_lr", "gla_b_lr", "gla_norm_g", "s5_lam_re", "s5_lam_im", "s5_log_dt", "s5_b_re", "s5_b_im",
           "s5_c_re", "s5_c_im", "s5_d", "s5_w_glu", "s5_b_glu", "w_branch", "w_mix_out", "ln1_g", "ln1_b", "xa_w_q",
           "xa_w_kv", "xa_w_o", "ln2_g", "ln2_b", "ffn_w_gu", "ffn_w_down", "ln3_g", "ln3_b"]


def kernel(**inputs):
    x = np.ascontiguousarray(inputs["x"], dtype=np.float32)
    mem = np.ascontiguousarray(inputs["mem"], dtype=np.float32)
    B, L, _ = x.shape
    nc = build(n_tiles=L // TT, ncores=2 * B)
    per_layer = []
    for l in range(DEPTH):
        w = {}
        for k in _WNAMES:
            a = np.asarray(inputs[k], dtype=np.float32)
            w[k] = np.ascontiguousarray(a) if k in ("ln_in_g", "ln_in_b") else np.ascontiguousarray(a[l:l + 1])
        per_layer.append(w)
    zeros_x = np.zeros((L, D), np.float32)
    in_maps = []
    for b in range(B):
        for l in range(DEPTH):
            m = dict(per_layer[l])
            m["x"] = x[b] if l == 0 else zeros_x
            m["mem"] = mem[b]
            m["role"] = np.full((128, 1), 1.0 if l == 0 else 0.0, np.float32)
            in_maps.append(m)
    res = run_bass_kernel_spmd(nc, in_maps, core_ids=list(range(2 * B)))
    return np.stack([res.results[2 * b + 1]["out"] for b in range(B)], axis=0)
```

```python
import math
import contextlib
import numpy as np
import concourse.bass as bass
import concourse.mybir as mybir
from concourse.bass_utils import run_bass_kernel_spmd

F32 = mybir.dt.float32
BF16 = mybir.dt.bfloat16
AF = mybir.ActivationFunctionType
ALU = mybir.AluOpType

D = 1024
TT = 512
NMEM = 256
DEPTH = 2
LAG = 1
NL = 1
D_IN = 6160
DFF = 2816
ALPHA = (2 * DEPTH) ** 0.25
EPS = 1e-5
PI = math.pi
SAME_ENG_SYNC = True
SENT = 1 << 40
MAGIC = 12582912.0
TWO_PI_S = 2 * math.pi * (1 - 2e-7)


class V:
    __slots__ = ("buf", "ap")

    def __init__(self, buf, ap):
        self.buf = buf
        self.ap = ap


class Buf:
    def __init__(self, t, name):
        self.t = t
        self.name = name
        self.w = None
        self.rs = []
        self.dsem = None

    def __getitem__(self, idx):
        return V(self, self.t[idx])


class Node:
    __slots__ = ("idx", "eng", "fn", "deps", "succ", "dur", "kind", "sembuf", "nbytes", "nun", "ready",
                 "start", "end", "sem", "val", "clk", "done", "tag", "inc", "afam")

    def __init__(self, idx, eng, fn, deps, dur, kind, sembuf=None, nbytes=0):
        self.idx = idx; self.eng = eng; self.fn = fn; self.deps = deps; self.succ = []
        self.dur = dur; self.kind = kind; self.sembuf = sembuf; self.nbytes = nbytes
        self.nun = 0; self.ready = 0.0; self.start = None; self.end = None
        self.sem = None; self.val = None; self.clk = None; self.done = False; self.inc = 16; self.afam = None


ENGS = ("tensor", "vector", "scalar", "gpsimd", "sync")
AFAM = {AF.Exp: "lnexp", AF.Ln: "lnexp", AF.Sigmoid: "sig", AF.Sqrt: "sqrt", AF.Sin: "sin", AF.Gelu: "gelu", AF.Silu: "silu"}
ATL = 1.28
SELF_DRAIN = 0.25
SELFSYNC = {"tensor": False, "vector": SAME_ENG_SYNC, "scalar": SAME_ENG_SYNC, "gpsimd": SAME_ENG_SYNC, "sync": False}
DMA_BW = 150e3
WINDOW = 64


class Ctx:
    def __init__(self, nc, stack):
        self.nc = nc
        self.stack = stack
        self.sems = {nm: stack.enter_context(nc.semaphore("sem_" + nm)) for nm in ENGS}
        self.nodes = []
        self.tag = "setup"
        self.itn = -1
        self.nbuf = 0
        self.setup_sem = None
        self.setup_nodes = []
        self.groups = []
        self.final = None

    def sbuf(self, shape, dtype, name=None):
        self.nbuf += 1
        name = name or f"b{self.nbuf}"
        t = self.stack.enter_context(self.nc.sbuf_tensor(name, list(shape), dtype))
        return Buf(t, name)

    def psum(self, name):
        t = self.stack.enter_context(self.nc.psum_tensor(name, [128, 512], F32))
        return Buf(t, name)

    def dram(self, name, shape, dtype, kind="Internal"):
        t = self.nc.dram_tensor(name, list(shape), dtype, kind=kind)
        b = Buf(t, name)
        b.apv = t.ap()
        return b

    def _mk(self, eng, fn, reads, writes, dur, kind, sembuf=None, nbytes=0, group=None):
        rb = [x.buf if isinstance(x, V) else x for x in reads]
        wb = [x.buf if isinstance(x, V) else x for x in writes]
        deps = set()
        for b in rb:
            if b.w is not None:
                deps.add(b.w)
        if group is not None and "wdeps" in group:
            deps |= group["wdeps"]
        else:
            wd = set()
            for b in wb:
                for r in b.rs:
                    wd.add(r)
                if b.w is not None:
                    wd.add(b.w)
            if group is not None:
                group["wdeps"] = wd
            deps |= wd
        n = Node(len(self.nodes), eng, fn, deps, dur, kind, sembuf, nbytes)
        n.tag = self.tag + "@" + str(self.itn)
        deps.discard(n)
        self.nodes.append(n)
        for b in rb:
            b.rs.append(n)
        for b in wb:
            b.w = n
            b.rs = []
        return n

    def op(self, engname, fn, reads, writes, dur=0.6):
        return self._mk(engname, fn, reads, writes, dur, "op")

    def dma(self, qname, fn, reads, writes, sembuf, nbytes=1 << 20, group=None, inc=16):
        if sembuf.dsem is None:
            sembuf.dsem = self.stack.enter_context(self.nc.semaphore("ds_" + sembuf.name))
        n = self._mk(qname, fn, reads, writes, 0.1, "dma", sembuf, nbytes, group)
        n.inc = inc
        if group is not None:
            group.setdefault("nodes", []).append(n)
            self.groups.append(group) if group.get("reg") is None else None
            group["reg"] = True
        return n

    def dma_setup(self, fn, out):
        if self.setup_sem is None:
            self.setup_sem = self.stack.enter_context(self.nc.semaphore("ds_setup"))
        n = Node(len(self.nodes), "sync", fn, set(), 0.1, "setup", None, 4096)
        n.tag = "setup"
        self.nodes.append(n)
        self.setup_nodes.append(n)
        out.buf.w = n
        out.buf.rs = []
        return n

    def final_wait(self, qname, toks):
        self.final = (qname, toks)

    def schedule(self):
        nodes = self.nodes
        for n in nodes:
            n.nun = len(n.deps)
            for d in n.deps:
                d.succ.append(n)
        queues = {e: [n for n in nodes if n.eng == e] for e in ENGS}
        pos = {e: 0 for e in ENGS}
        win = {e: [] for e in ENGS}
        nxt = {e: 0 for e in ENGS}
        tfree = {e: 0.0 for e in ENGS}
        dma_free = 0.0
        order = []
        total = len(nodes)
        cur_fam = [None]

        def refill(e):
            q = queues[e]
            w = win[e]
            while len(w) < WINDOW and nxt[e] < len(q):
                w.append(q[nxt[e]])
                nxt[e] += 1

        for e in ENGS:
            refill(e)
        while len(order) < total:
            best = None
            bs = None
            for e in ENGS:
                tf = tfree[e]
                for n in win[e]:
                    if n.nun == 0:
                        st = n.ready if n.ready > tf else tf
                        if n.afam is not None and n.afam != cur_fam[0]:
                            st += ATL
                        if bs is None or st < bs or (st == bs and n.idx < best.idx):
                            bs = st
                            best = n
            assert best is not None, "scheduler deadlock (cyclic deps?)"
            n = best
            e = n.eng
            if n.afam is not None:
                cur_fam[0] = n.afam
            n.start = bs
            if n.kind == "op":
                n.end = bs + n.dur
                tfree[e] = n.end
            else:
                tfree[e] = bs + n.dur
                xs = dma_free if dma_free > bs else bs
                dma_free = xs + n.nbytes / DMA_BW
                n.end = dma_free + 2.0
            win[e].remove(n)
            refill(e)
            for sn in n.succ:
                sn.nun -= 1
                rdy = n.end + (SELF_DRAIN if (sn.eng == e and SELFSYNC[e] and n.kind == "op") else 0.0)
                if rdy > sn.ready:
                    sn.ready = rdy
            order.append(n)
        self.order = order
        self.est_us = max(n.end for n in nodes)

    def emit(self):
        order = self.order
        cnt = {e: 0 for e in ENGS}
        dcnt = {}
        n_setup = len(self.setup_nodes)
        for n in order:
            e = n.eng
            if n.kind == "op":
                cnt[e] += 1
                n.sem = self.sems[e]
                n.val = cnt[e]
                n.inc = 1
            elif n.kind == "dma":
                sb = n.sembuf
                dcnt[id(sb)] = dcnt.get(id(sb), 0) + 1
                n.sem = sb.dsem
                n.val = n.inc * dcnt[id(sb)]
            else:
                n.sem = self.setup_sem
                n.val = 16 * n_setup
                n.inc = 16
        for g in self.groups:
            gv = max(m.val for m in g["nodes"])
            for m in g["nodes"]:
                m.val = gv
        known = {e: {} for e in ENGS}
        streams = {e: [] for e in ENGS}
        engsem = {id(self.sems[e]) for e in ENGS}
        for n in order:
            e = n.eng
            kn = known[e]
            need = {}
            for d in n.deps:
                k = id(d.sem)
                if k not in need or need[k][1] < d.val:
                    need[k] = (d.sem, d.val, d.clk)
            waits = []
            for k, (sem, val, clk) in need.items():
                if kn.get(k, 0) >= val:
                    continue
                if sem is self.sems[e] and not SELFSYNC[e]:
                    continue
                waits.append((sem, val))
                kn[k] = val
                if clk:
                    for kk, vv in clk.items():
                        if kn.get(kk, 0) < vv:
                            kn[kk] = vv
            if n.kind == "op":
                clk = {k: v for k, v in kn.items() if k in engsem}
                clk[id(n.sem)] = n.val
                n.clk = clk
            else:
                n.clk = None
            streams[e].append((waits, n.fn, n.sem, n.inc))
        if self.final is not None:
            q, toks = self.final
            streams[q].append(([(t.sem, t.val) for t in toks], None, None, 0))
        self.streams = streams

    def replay(self, engname, e):
        for (waits, fn, sem, inc) in self.streams[engname]:
            for (s, v) in waits:
                e.wait_ge(s, v)
            if fn is not None:
                fn(e).then_inc(sem, inc)


class Pool:
    def __init__(self, bufs):
        self.free = list(bufs)
        self.all = list(bufs)

    def get(self):
        assert self.free, "pool exhausted"
        return self.free.pop(0)

    def put(self, b):
        assert b in self.all and b not in self.free
        self.free.append(b)


def build(n_tiles=8, n_stage=7, ncores=8):
    nc = bass.Bass("TRN2", target_bir_lowering=False)
    L = n_tiles * TT
    stack = contextlib.ExitStack()
    with stack:
        cx = Ctx(nc, stack)
        _build_body(nc, cx, n_tiles, n_stage, L, [[2 * i, 2 * i + 1] for i in range(ncores // 2)])
        cx.schedule()
        cx.emit()
        build.last_est_us = cx.est_us
        with nc.Block() as block:
            @block.tensor
            def _(e):
                cx.replay("tensor", e)

            @block.vector
            def _(e):
                cx.replay("vector", e)

            @block.scalar
            def _(e):
                cx.replay("scalar", e)

            @block.gpsimd
            def _(e):
                cx.replay("gpsimd", e)

            @block.sync
            def _(e):
                cx.replay("sync", e)
    return nc


def _build_body(nc, cx, n_tiles, n_stage, L, RG):
    def din(name, shape):
        b = cx.dram(name, shape, F32, kind="ExternalInput")
        return b

    x_d = din("x", [L, D])
    mem_d = din("mem", [NMEM, D])
    ln_in_g = din("ln_in_g", [D]); ln_in_b = din("ln_in_b", [D])
    w_in = din("w_in", [NL, D, D_IN]); b_in = din("b_in", [NL, D_IN])
    conv_w = din("lru_conv_w", [NL, 4, 512]); conv_b = din("lru_conv_b", [NL, 512])
    lru_w_a = din("lru_w_a", [NL, 4, 128, 128]); lru_b_a = din("lru_b_a", [NL, 512])
    lru_w_i = din("lru_w_i", [NL, 4, 128, 128]); lru_b_i = din("lru_b_i", [NL, 512])
    lru_lam = din("lru_lambda", [NL, 512])
    gla_w_lr = din("gla_w_lr", [NL, 16, 256]); gla_b_lr = din("gla_b_lr", [NL, 256])
    gla_g = din("gla_norm_g", [NL, 128])
    s5_lre = din("s5_lam_re", [NL, 32, 64]); s5_lim = din("s5_lam_im", [NL, 32, 64])
    s5_ldt = din("s5_log_dt", [NL, 32])
    s5_bre = din("s5_b_re", [NL, 32, 64, 16]); s5_bim = din("s5_b_im", [NL, 32, 64, 16])
    s5_cre = din("s5_c_re", [NL, 32, 16, 64]); s5_cim = din("s5_c_im", [NL, 32, 16, 64])
    s5_d = din("s5_d", [NL, 512]); s5_wglu = din("s5_w_glu", [NL, 512, 512]); s5_bglu = din("s5_b_glu", [NL, 512])
    w_branch = din("w_branch", [NL, 3, 512, D]); w_mix = din("w_mix_out", [NL, D, D])
    ln1_g = din("ln1_g", [NL, D]); ln1_b = din("ln1_b", [NL, D])
    xa_wq = din("xa_w_q", [NL, D, D]); xa_wkv = din("xa_w_kv", [NL, D, 2 * D]); xa_wo = din("xa_w_o", [NL, D, D])
    ln2_g = din("ln2_g", [NL, D]); ln2_b = din("ln2_b", [NL, D])
    w_gu = din("ffn_w_gu", [NL, D, 2 * DFF]); w_down = din("ffn_w_down", [NL, DFF, D])
    ln3_g = din("ln3_g", [NL, D]); ln3_b = din("ln3_b", [NL, D])
    role_d = din("role", [128, 1])
    out_d = cx.dram("out", [L, D], F32, kind="ExternalOutput")
    snd_d = [cx.dram(f"snd{i}", [128, 8 * TT], F32, kind="Internal") for i in range(3)]
    gat_d = [cx.dram(f"gat{i}", [256, 8 * TT], F32, kind="Internal") for i in range(2)]

    def scr(name, shape):
        return cx.dram(name, shape, BF16, kind="Internal")

    w_in_s = scr("w_in_s", [NL, D, D_IN])
    wa_s = scr("wa_s", [NL, 512, 128]); wi_s = scr("wi_s", [NL, 512, 128])
    wglu_s = scr("wglu_s", [NL, 512, 512])
    wbr_s = scr("wbr_s", [NL, 1536, D]); wmix_s = scr("wmix_s", [NL, D, D])
    wq_s = scr("wq_s", [NL, D, D]); wo_s = scr("wo_s", [NL, D, D])
    wgu_s = scr("wgu_s", [NL, D, 2 * DFF]); wdn_s = scr("wdn_s", [NL, DFF, D])
    s5w_s = scr("s5w_s", [NL, 128, 80 * 128])
    tab_s = cx.dram("tab_s", [NL, 16, 128, 2 * TT], F32, kind="Internal")
    kt_s = scr("kt_s", [NL, 128, 8 * 256])
    v_s = scr("v_s", [NL, 128, 2 * 1024])

    dve = "vector"; act = "scalar"; pool = "gpsimd"; pe = "tensor"; sp = "sync"

    def nel(ap):
        n = 1
        for d_ in ap.shape[1:]:
            n *= d_
        return n

    def is_ps(v):
        return v.buf.name.startswith("ps")

    def edur(eng, n, psrc=False, mult=1.0):
        if eng == dve:
            return (0.12 + n / 700.0 * (1.15 if psrc else 1.0)) * mult
        if eng == pool:
            return (0.3 + n / 420.0) * mult
        return (0.25 + n / 1350.0) * mult

    def mm(out, lhsT, rhs, start, stop):
        cols = nel(rhs.ap)
        d_ = 0.004 + max(cols, 64) / 2400.0 * (4 if rhs.ap.dtype == F32 else 1)
        cx.op(pe, lambda e: e.matmul(out.ap, lhsT.ap, rhs.ap, start=start, stop=stop), [lhsT, rhs], [out], d_)

    def transpose(out, in_, ident):
        cx.op(pe, lambda e: e.transpose(out.ap, in_.ap, ident.ap), [in_, ident], [out], 0.31)

    def actf(out, in_, func, bias=None, scale=None, eng=act):
        kw = {}
        rd = [in_]
        if bias is not None:
            if isinstance(bias, V):
                kw["bias"] = bias.ap; rd.append(bias)
            else:
                kw["bias"] = float(bias)
        if scale is not None:
            if isinstance(scale, V):
                kw["scale"] = scale.ap; rd.append(scale)
            else:
                kw["scale"] = float(scale)
        nd = cx.op(act, lambda e: e.activation(out=out.ap, in_=in_.ap, func=func, **kw), rd, [out], edur(act, nel(out.ap)))
        nd.afam = AFAM.get(func)

    def tt(eng, out, in0, in1, op):
        cx.op(eng, lambda e: e.tensor_tensor(out=out.ap, in0=in0.ap, in1=in1.ap, op=op), [in0, in1], [out],
              edur(eng, nel(out.ap), is_ps(in0) or is_ps(in1)))

    def ts(eng, out, in0, s1, s2, op0, op1=None):
        rd = [in0]
        a1 = s1
        if isinstance(s1, V):
            rd.append(s1); a1 = s1.ap
        a2 = s2
        if isinstance(s2, V):
            rd.append(s2); a2 = s2.ap
        d_ = edur(eng, nel(out.ap), is_ps(in0), 5.5 if eng == pool else 1.0)
        if op1 is None:
            cx.op(eng, lambda e: e.tensor_scalar(out=out.ap, in0=in0.ap, scalar1=a1, scalar2=None, op0=op0), rd, [out], d_)
        else:
            cx.op(eng, lambda e: e.tensor_scalar(out=out.ap, in0=in0.ap, scalar1=a1, scalar2=a2, op0=op0, op1=op1), rd, [out], d_)

    def stt(eng, out, in0, scalar, in1, op0, op1):
        eng = dve
        rd = [in0, in1]
        a = scalar
        if isinstance(scalar, V):
            rd.append(scalar); a = scalar.ap
        cx.op(eng, lambda e: e.scalar_tensor_tensor(out=out.ap, in0=in0.ap, scalar=a, in1=in1.ap, op0=op0, op1=op1), rd, [out],
              edur(dve, nel(out.ap), is_ps(in0) or is_ps(in1)))

    def scan(out, d0, d1, initial, op0=ALU.mult, op1=ALU.add):
        rd = [d0, d1]
        a = initial
        if isinstance(initial, V):
            rd.append(initial); a = initial.ap
        cx.op(dve, lambda e: e.tensor_tensor_scan(out=out.ap, data0=d0.ap, data1=d1.ap, initial=a, op0=op0, op1=op1), rd, [out],
              edur(dve, nel(out.ap), False, 2.0))

    def recip(out, in_):
        cx.op(dve, lambda e: e.reciprocal(out=out.ap, in_=in_.ap), [in_], [out], edur(dve, nel(out.ap), is_ps(in_), 4.3))

    def rsqrt_eps(out, in_):
        actf(out, in_, AF.Ln, bias=EPS)
        actf(out, out, AF.Exp, scale=-0.5)

    def copy(eng, out, in_):
        if eng == act:
            cx.op(act, lambda e: e.copy(out=out.ap, in_=in_.ap), [in_], [out], edur(act, nel(out.ap)))
        else:
            cx.op(eng, lambda e: e.tensor_copy(out=out.ap, in_=in_.ap), [in_], [out], edur(eng, nel(out.ap), is_ps(in_)))

    def memset(eng, out, val):
        cx.op(eng, lambda e: e.memset(out.ap, float(val)), [], [out], edur(eng, nel(out.ap)) * 0.5)

    def dma(q, out, in_, sembuf=None, track_out=True, group=None, **kw):
        sb = sembuf or out.buf
        nb = out.ap.shape[0] * nel(out.ap) * (2 if out.ap.dtype == BF16 else 4)
        return cx.dma(q, lambda e: e.dma_start(out=out.ap, in_=in_.ap, **kw), [in_], [out] if track_out else [], sb, nb, group)

    def cc_gather(snd, gat):
        return cx.dma(pool, lambda e: e.collective_compute("AllGather", ALU.bypass, replica_groups=RG,
                                                            ins=[snd.apv], outs=[gat.apv]),
                      [snd], [gat], gat, 9 << 20, None, 1)

    def dma_setup(out, in_, **kw):
        cx.dma_setup(lambda e: e.dma_start(out=out.ap, in_=in_.ap, **kw), out)

    psum = Pool([cx.psum(f"ps{i}") for i in range(8)])
    NSLAB = 6
    slabs = Pool([cx.sbuf([128, 4096], BF16, f"slab{i}") for i in range(NSLAB)])
    ftmp = Pool([cx.sbuf([128, TT], F32, f"ft{i}") for i in range(21)])
    tabs = Pool([cx.sbuf([128, 2 * TT], F32, f"tab{i}") for i in range(2)])
    htmp = Pool([cx.sbuf([128, TT], BF16, f"ht{i}") for i in range(30)])

    hS = [[cx.sbuf([128, TT], F32, f"hS{i}_{k}") for k in range(8)] for i in range(2)]
    hbs = [[cx.sbuf([128, TT], BF16, f"hb{i}_{k}") for k in range(8)] for i in range(2)]
    cur = {"hb": hbs[0]}

    ident = cx.sbuf([128, 128], F32, "ident")
    onesf = cx.sbuf([128, 128], F32, "onesf")
    memset(pool, onesf[:], 1.0)
    cx.op(pool, lambda e: e.affine_select(out=ident.t[:], in_=onesf.t[:], pattern=[[-1, 128]], compare_op=ALU.is_equal,
                                          fill=0.0, base=0, channel_multiplier=1), [onesf], [ident])
    ones_d = cx.sbuf([128, 128], BF16, "ones_d")
    memset(pool, ones_d[:], 1.0 / 1024)
    ones_v = cx.sbuf([128, 128], BF16, "ones_v")
    memset(pool, ones_v[:], 1.0 / 128)
    ones_1 = cx.sbuf([128, 128], BF16, "ones_1")
    memset(pool, ones_1[:], 1.0)
    mask4 = cx.sbuf([128, 512], F32, "mask4")
    onesw = cx.sbuf([128, 128], F32, "onesw")
    memset(pool, onesw[:], 1.0)
    for n in range(4):
        cx.op(pool, lambda e, n=n: e.affine_select(out=mask4.t[:, n * 128:(n + 1) * 128], in_=onesw.t[:, 0:128], pattern=[[1, 128]],
                                                   compare_op=ALU.is_ge, fill=0.0, base=0, channel_multiplier=-1), [onesw], [mask4])
    cmask = cx.sbuf([128, 512], F32, "cmask")
    memset(pool, cmask[:], 1.0)
    for n in range(4):
        memset(pool, cmask[:, n * 128:n * 128 + 1], 0.0)
    iota = ftmp.get()
    cx.op(pool, lambda e: e.iota(iota.t[:], pattern=[[1, 512]], base=0, channel_multiplier=0,
                                 allow_small_or_imprecise_dtypes=True), [], [iota])

    def col_load(name, src_ap, shape):
        b = cx.sbuf(shape, F32, name)
        dma_setup(b[:], V(src_ap[0], src_ap[1]), allow_slow_non_contiguous=True)
        return b

    def cols(dt, l, off, n, name):
        ap = dt.apv[l, off:off + n * 128] if l is not None else dt.apv[off:off + n * 128]
        return col_load(name, (dt, ap.rearrange("(j p) -> p j", p=128)), [128, n])

    g_in = cols(ln_in_g, None, 0, 8, "g_in"); b_in_ln = cols(ln_in_b, None, 0, 8, "b_in_ln")
    sel = cx.sbuf([128, 1], F32, "sel")
    dma_setup(sel[:], V(role_d, role_d.apv))
    nsel = cx.sbuf([128, 1], F32, "nsel")
    ts(dve, nsel[:], sel[:], -1.0, 1.0, ALU.mult, ALU.add)
    ts(dve, g_in[:], g_in[:], sel[:, 0:1], None, ALU.mult)
    ts(dve, b_in_ln[:], b_in_ln[:], sel[:, 0:1], None, ALU.mult)
    LP = []
    for l in range(NL):
        p = {}
        p["b_main"] = cols(b_in, l, 0, 20, f"bmain{l}")
        p["b_s5"] = cols(b_in, l, 2576, 4, f"bs5{l}")
        p["b_gate"] = cols(b_in, l, 3088, 24, f"bgate{l}")
        p["b_lr"] = col_load(f"blr{l}", (b_in, b_in.apv[l, 2560:2576].rearrange("(p o) -> p o", o=1)), [16, 1])
        p["b_qk"] = col_load(f"bqk{l}", (b_in, b_in.apv[l, 1024:1536].rearrange("(j p) -> p j", p=64)), [64, 8])
        p["bq8"] = cx.sbuf([64, 4], F32, f"bq8{l}")
        ts(pool, p["bq8"][:], p["b_qk"][:, 0:4], 0.125, None, ALU.mult)
        p["b_v"] = cx.sbuf([128, 512], F32, f"bv{l}")
        dma_setup(p["b_v"][:], V(b_in, b_in.apv[l:l + 1, 1536:2048].broadcast_to([128, 512])))
        p["ln1g"] = cols(ln1_g, l, 0, 8, f"ln1g{l}"); p["ln1b"] = cols(ln1_b, l, 0, 8, f"ln1b{l}")
        p["ln2g"] = cols(ln2_g, l, 0, 8, f"ln2g{l}"); p["ln2b"] = cols(ln2_b, l, 0, 8, f"ln2b{l}")
        p["ln3g"] = cols(ln3_g, l, 0, 8, f"ln3g{l}"); p["ln3b"] = cols(ln3_b, l, 0, 8, f"ln3b{l}")
        p["convw"] = col_load(f"convw{l}", (conv_w, conv_w.apv[l].rearrange("k (c p) -> p k c", p=128)), [128, 4, 4])
        p["convb"] = cols(conv_b, l, 0, 4, f"convb{l}")
        p["b_a"] = cols(lru_b_a, l, 0, 4, f"ba{l}"); p["b_i"] = cols(lru_b_i, l, 0, 4, f"bi{l}")
        lam = cols(lru_lam, l, 0, 4, f"lam{l}")
        e1 = cx.sbuf([128, 4], F32, f"lrue{l}")
        actf(e1[:], lam[:], AF.Exp, scale=-1.0)
        actf(e1[:], e1[:], AF.Ln, bias=1.0)
        p["cA"] = cx.sbuf([128, 4], F32, f"cA{l}"); p["cA2"] = cx.sbuf([128, 4], F32, f"cA2{l}")
        ts(pool, p["cA"][:], e1[:], -8.0, None, ALU.mult)
        ts(pool, p["cA2"][:], e1[:], -16.0, None, ALU.mult)
        nb = col_load(f"nblr{l}", (gla_b_lr, gla_b_lr.apv[l].rearrange("(j p) -> p j", p=64)), [64, 4])
        p["nb_lr"] = cx.sbuf([64, 4], F32, f"nblr2{l}")
        ts(pool, p["nb_lr"][:], nb[:], -1.0, None, ALU.mult)
        p["w_lr"] = cx.sbuf([16, 256], BF16, f"wlr{l}")
        dma(pool, p["w_lr"][:], V(gla_w_lr, gla_w_lr.apv[l]))
        p["gn"] = col_load(f"gn{l}", (gla_g, gla_g.apv[l].rearrange("(p o) -> p o", o=1)), [128, 1])
        p["s5d"] = cols(s5_d, l, 0, 4, f"s5d{l}"); p["bglu"] = cols(s5_bglu, l, 0, 4, f"bglu{l}")
        lre = cx.sbuf([128, 16], F32, f"lre{l}"); lim = cx.sbuf([128, 16], F32, f"lim{l}"); ldt = cx.sbuf([128, 16], F32, f"ldt{l}")
        for two in range(2):
            dma_setup(lre[two * 64:(two + 1) * 64, :], V(s5_lre, s5_lre.apv[l].rearrange("(j two) p -> two p j", two=2)[two]),
                allow_slow_non_contiguous=True)
            dma_setup(lim[two * 64:(two + 1) * 64, :], V(s5_lim, s5_lim.apv[l].rearrange("(j two) p -> two p j", two=2)[two]),
                allow_slow_non_contiguous=True)
            dma_setup(ldt[two * 64:(two + 1) * 64, :],
                V(s5_ldt, s5_ldt.apv[l].rearrange("(j two) -> two j", two=2)[two:two + 1, :].broadcast_to([64, 16])),
                allow_slow_non_contiguous=True)

        p["lre"] = lre; p["lim"] = lim; p["ldt"] = ldt
        LP.append(p)

    cast_toks = {}

    def cast(dst, src, l, rows, cols_, a=1):
        if a == 1:
            s_ap = src.apv[l]; d_ap = dst.apv[l]
            if len(s_ap.shape) > 2:
                s_ap = s_ap.flatten_outer_dims()
        else:
            s_ap = src.apv[l].rearrange("k (a c) -> (k a) c", a=a)
            d_ap = dst.apv[l].rearrange("k (a c) -> (k a) c", a=a)
        dma(pool, V(dst, d_ap), V(src, s_ap), sembuf=dst)

    for l in range(NL):
        cast(w_in_s, w_in, l, D, D_IN, a=5)
        cast(wa_s, lru_w_a, l, 512, 128); cast(wi_s, lru_w_i, l, 512, 128)
        cast(wglu_s, s5_wglu, l, 512, 512)
        cast(wbr_s, w_branch, l, 1536, D); cast(wmix_s, w_mix, l, D, D)
        cast(wq_s, xa_wq, l, D, D); cast(wo_s, xa_wo, l, D, D)
        cast(wgu_s, w_gu, l, D, 2 * DFF, a=4); cast(wdn_s, w_down, l, DFF, D)

    stg = cx.sbuf([128, 128], BF16, "s5stg0"), cx.sbuf([128, 128], BF16, "s5stg1")
    nstg = [0]
    EBr = cx.sbuf([128, 128], F32, "EBr"); EBi = cx.sbuf([128, 128], F32, "EBi")
    ECr = cx.sbuf([128, 128], F32, "ECr"); ECi = cx.sbuf([128, 128], F32, "ECi")
    Et = cx.sbuf([128, 128], F32, "Et"); Eo = cx.sbuf([128, 128], F32, "Eo")
    for l in range(NL):
        p = LP[l]
        lre = p["lre"]; lim = p["lim"]; ldt = p["ldt"]

        def t16(nm):
            return cx.sbuf([128, 16], F32, f"{nm}{l}")

        dt_ = t16("dt"); actf(dt_[:], ldt[:], AF.Exp)
        th = t16("th"); tt(dve, th[:], lim[:], dt_[:], ALU.mult)
        lrd = t16("lrd"); tt(dve, lrd[:], lre[:], dt_[:], ALU.mult)
        rho = t16("rho"); actf(rho[:], lrd[:], AF.Exp)
        cu = t16("cu"); ts(dve, cu[:], th[:], 1.0 / (2 * PI), None, ALU.mult)
        ck = t16("ck"); ts(dve, ck[:], cu[:], MAGIC, None, ALU.add)
        ts(dve, ck[:], ck[:], -MAGIC, None, ALU.add)
        cfr = t16("cfr"); tt(dve, cfr[:], cu[:], ck[:], ALU.subtract)
        snt = t16("snt"); actf(snt[:], cfr[:], AF.Sin, scale=TWO_PI_S)
        ac = t16("ac"); actf(ac[:], cfr[:], AF.Abs)
        cst = t16("cst"); actf(cst[:], ac[:], AF.Sin, bias=PI / 2, scale=-TWO_PI_S)
        abr = t16("abr"); tt(dve, abr[:], rho[:], cst[:], ALU.mult)
        abi = t16("abi"); tt(dve, abi[:], rho[:], snt[:], ALU.mult)
        abm = t16("abm"); ts(dve, abm[:], abr[:], -1.0, None, ALU.add)
        den = t16("den"); tt(dve, den[:], lre[:], lre[:], ALU.mult)
        t2 = t16("t2"); tt(dve, t2[:], lim[:], lim[:], ALU.mult)
        tt(dve, den[:], den[:], t2[:], ALU.add)
        rden = t16("rden"); recip(rden[:], den[:])
        fre = t16("fre"); fim = t16("fim"); t3 = t16("t3")
        tt(dve, fre[:], abm[:], lre[:], ALU.mult); tt(dve, t3[:], abi[:], lim[:], ALU.mult)
        tt(dve, fre[:], fre[:], t3[:], ALU.add); tt(dve, fre[:], fre[:], rden[:], ALU.mult)
        tt(dve, fim[:], abi[:], lre[:], ALU.mult); tt(dve, t3[:], abm[:], lim[:], ALU.mult)
        tt(dve, fim[:], fim[:], t3[:], ALU.subtract); tt(dve, fim[:], fim[:], rden[:], ALU.mult)
        nsnt = t16("nsnt"); ts(dve, nsnt[:], snt[:], -1.0, None, ALU.mult)
        uT = t16("uT"); ts(dve, uT[:], cfr[:], float(TT), None, ALU.mult)
        kT = t16("kT"); ts(dve, kT[:], uT[:], MAGIC, None, ALU.add)
        ts(dve, kT[:], kT[:], -MAGIC, None, ALU.add)
        fT = t16("fT"); tt(dve, fT[:], uT[:], kT[:], ALU.subtract)
        snT = t16("snT"); actf(snT[:], fT[:], AF.Sin, scale=TWO_PI_S)
        aT = t16("aT"); actf(aT[:], fT[:], AF.Abs)
        csT = t16("csT"); actf(csT[:], aT[:], AF.Sin, bias=PI / 2, scale=-TWO_PI_S)
        nsnT = t16("nsnT"); ts(dve, nsnT[:], snT[:], -1.0, None, ALU.mult)
        p.update(th=th, rho=rho, cst=cst, snt=snt, nsnt=nsnt, cfr=cfr, snT=snT, csT=csT, nsnT=nsnT)
        for j in range(16):
            tb = tabs.get(); kf = ftmp.get(); fa = ftmp.get()
            ts(dve, kf[:], iota[:], cfr[:, j:j + 1], MAGIC, ALU.mult, ALU.add)
            actf(kf[:], kf[:], AF.Identity, bias=-MAGIC)
            stt(dve, fa[:], iota[:], cfr[:, j:j + 1], kf[:], ALU.mult, ALU.subtract)
            actf(tb[:, TT:2 * TT], fa[:], AF.Sin, scale=TWO_PI_S)
            actf(fa[:], fa[:], AF.Abs)
            actf(tb[:, 0:TT], fa[:], AF.Sin, bias=PI / 2, scale=-TWO_PI_S)
            dma(sp, V(tab_s, tab_s.apv[l, j]), tb[:], sembuf=tab_s)
            tabs.put(tb); ftmp.put(kf); ftmp.put(fa)

        for j in range(16):
            for E_ in (EBr, EBi, ECr, ECi):
                memset(pool, E_[:], 0.0)
            for two in range(2):
                g = 2 * j + two
                co = (g % 8) * 16
                dma(sp, EBr[two * 64:(two + 1) * 64, co:co + 16], V(s5_bre, s5_bre.apv[l, g]))
                dma(sp, EBi[two * 64:(two + 1) * 64, co:co + 16], V(s5_bim, s5_bim.apv[l, g]))
                dma(sp, ECr[co:co + 16, two * 64:(two + 1) * 64], V(s5_cre, s5_cre.apv[l, g]))
                dma(sp, ECi[co:co + 16, two * 64:(two + 1) * 64], V(s5_cim, s5_cim.apv[l, g]))
            ts(dve, Et[:], EBi[:], fim[:, j:j + 1], None, ALU.mult)
            stt(dve, Eo[:], EBr[:], fre[:, j:j + 1], Et[:], ALU.mult, ALU.subtract)
            def emit(srcE, idx, sc):
                ps = psum.get()
                transpose(ps[:, 0:128], srcE[:], ident[:])
                sg = stg[nstg[0] % 2]; nstg[0] += 1
                actf(sg[:], ps[:, 0:128], AF.Copy, scale=sc)
                psum.put(ps)
                dma(sp, V(s5w_s, s5w_s.apv[l, :, idx * 128:(idx + 1) * 128]), sg[:], sembuf=s5w_s)

            emit(Eo, 0 * 16 + j, 1.0)
            ts(dve, Et[:], EBr[:], fim[:, j:j + 1], None, ALU.mult)
            stt(dve, Eo[:], EBi[:], fre[:, j:j + 1], Et[:], ALU.mult, ALU.add)
            emit(Eo, 1 * 16 + j, 1.0)
            emit(ECr, 2 * 16 + j, 1.0)
            emit(ECi, 3 * 16 + j, -1.0)
            emit(ECr, 4 * 16 + j, -1.0)

    ftmp.put(iota)
    kvsw = cx.sbuf([1, 2], F32, "kvsw")
    memT_b = [ftmp.get(), ftmp.get()]

    def memT(k, lo=0, hi=256):
        bb = memT_b[k // 4]
        return V(bb, bb.t[:].bitcast(BF16).rearrange("p (k m) -> p k m", k=4)[:, k % 4, lo:hi])

    for mc in range(2):
        xh = [ftmp.get(), ftmp.get()]
        for hf in range(2):
            dma(sp, xh[hf][:], V(mem_d, mem_d.apv[mc * 128:(mc + 1) * 128, hf * 512:(hf + 1) * 512]))
        for k in range(8):
            ps = psum.get()
            transpose(ps[:, 0:128], xh[k // 4][:, (k % 4) * 128:(k % 4 + 1) * 128], ident[:])
            copy(act, memT(k, mc * 128, (mc + 1) * 128), ps[:, 0:128])
            psum.put(ps)
        ftmp.put(xh[0]); ftmp.put(xh[1])
    for l in range(NL):
        for q4 in range(4):
            sl = slabs.get()
            slv = sl.t[:].rearrange("p (k c) -> p k c", k=8)
            cx.dma(pool, lambda e, slv=slv, q4=q4, l=l: e.dma_start(
                       out=slv, in_=xa_wkv.apv[l].rearrange("(k p) c -> p k c", p=128)[:, :, q4 * 512:(q4 + 1) * 512]),
                   [xa_wkv], [sl, kvsw], kvsw, 2 << 20)
            if q4 < 2:
                stgk = htmp.get()
                for cc in range(4):
                    ps = psum.get()
                    for k in range(8):
                        mm(ps[:, 0:256], V(sl, slv[:, k, cc * 128:(cc + 1) * 128]), memT(k), k == 0, k == 7)
                    copy(act, stgk[:, (cc % 2) * 256:(cc % 2 + 1) * 256], ps[:, 0:256])
                    psum.put(ps)
                    if cc % 2 == 1:
                        c0 = q4 * 4 + cc - 1
                        dma(sp, V(kt_s, kt_s.apv[l, :, c0 * 256:(c0 + 2) * 256]), stgk[:], sembuf=kt_s)
                        if cc == 1:
                            htmp.put(stgk); stgk = htmp.get()
                htmp.put(stgk)
            else:
                for mc in range(2):
                    ps = psum.get()
                    for k in range(8):
                        mm(ps[:], memT(k, mc * 128, (mc + 1) * 128), V(sl, slv[:, k, :]), k == 0, k == 7)
                    stgv = htmp.get()
                    copy(act, stgv[:], ps[:])
                    psum.put(ps)
                    dma(sp, V(v_s, v_s.apv[l, :, mc * 1024 + (q4 - 2) * 512: mc * 1024 + (q4 - 1) * 512]), stgv[:], sembuf=v_s)
                    htmp.put(stgv)
            slabs.put(sl)

    ftmp.put(memT_b[0]); ftmp.put(memT_b[1])
    ST = []
    for l in range(NL):
        s = {}
        s["ubuf"] = [cx.sbuf([128, TT + 3], F32, f"ubuf{l}_{c}") for c in range(4)]
        s["hl"] = cx.sbuf([128, 4], F32, f"hlst{l}")
        memset(pool, s["hl"][:], 0.0)
        for c in range(4):
            memset(pool, s["ubuf"][c][:, TT:TT + 3], 0.0)
        s["S"] = [cx.sbuf([64, 128], F32, f"S{l}_{h}") for h in range(4)]
        s["Sb"] = [cx.sbuf([64, 128], BF16, f"Sb{l}_{h}") for h in range(4)]
        for h in range(4):
            memset(pool, s["S"][h][:], 0.0); memset(pool, s["Sb"][h][:], 0.0)
        s["zr"] = cx.sbuf([128, 16], F32, f"zr{l}"); s["zi"] = cx.sbuf([128, 16], F32, f"zi{l}")
        memset(pool, s["zr"][:], 0.0); memset(pool, s["zi"][:], 0.0)
        ST.append(s)

    def load_slab(src, src_ap, shape_str=None, **kw):
        sl = slabs.get()
        n = 1
        for d_ in src_ap.shape[1:]:
            n *= d_
        assert n <= 4096, n
        if len(src_ap.shape) == 3:
            view = sl.t[:, 0:n].rearrange("p (k c) -> p k c", k=src_ap.shape[1])
        else:
            view = sl.t[:, 0:n]
        dma(sp, V(sl, view), V(src, src_ap))
        return sl, view

    def kview(dt, l, c0, w, rows=None):
        ap = dt.apv[l] if rows is None else dt.apv[l, rows[0]:rows[1]]
        return ap.rearrange("(k p) c -> p k c", p=128)[:, :, c0:c0 + w]

    def layer_norm(r, hbo, gcol, bcol):
        cx.tag = "ln"
        rb = [htmp.get() for _ in range(8)]
        for k in range(8):
            copy(act, rb[k][:], r[k][:])
        mean = psum.get()
        for k in range(8):
            mm(mean[:], ones_d[:], rb[k][:], k == 0, k == 7)
        for k in range(8):
            tt(dve, r[k][:], r[k][:], mean[:], ALU.subtract)
        psum.put(mean)
        for k in range(8):
            actf(rb[k][:], r[k][:], AF.Square)
        var = psum.get()
        for k in range(8):
            mm(var[:], ones_d[:], rb[k][:], k == 0, k == 7)
        rstd = ftmp.get()
        rsqrt_eps(rstd[:], var[:])
        psum.put(var)
        for k in range(8):
            htmp.put(rb[k])
        for k in range(8):
            tt(dve, r[k][:], r[k][:], rstd[:], ALU.mult)
            actf(hbo[k][:], r[k][:], AF.Identity, bias=bcol[:, k:k + 1], scale=gcol[:, k:k + 1])
            actf(r[k][:], r[k][:], AF.Identity, bias=bcol[:, k:k + 1], scale=gcol[:, k:k + 1])
        ftmp.put(rstd)

    def proj_fm(ps, slab, view, col0, m, nk=8, rhs=None):
        rhs = rhs or cur["hb"]
        for k in range(nk):
            mm(ps[0:m, :], V(slab, view[:, k, col0:col0 + m]), rhs[k][:], k == 0, k == nk - 1)

    def mixer(l, hcur, rout, first_tile):
        p = LP[l]; s = ST[l]
        ya = [htmp.get() for _ in range(4)]

        def genA():
            cx.tag = "mixA"
            slu, vu = load_slab(w_in_s, kview(w_in_s, l, 0, 512))
            slg, vg = load_slab(w_in_s, kview(w_in_s, l, 512, 512))
            slw, vw = load_slab(wa_s, wa_s.apv[l].rearrange("(h k) m -> k h m", k=128))
            slw2, vw2 = load_slab(wi_s, wi_s.apv[l].rearrange("(h k) m -> k h m", k=128))
            for c in range(4):
                ub = s["ubuf"][c]
                copy(dve, ub[:, 0:3], ub[:, TT:TT + 3])
                ps = psum.get()
                proj_fm(ps, slu, vu, c * 128, 128)
                actf(ub[:, 3:TT + 3], ps[:], AF.Identity, bias=p["b_main"][:, c:c + 1])
                psum.put(ps)
                uc = ftmp.get()
                ts(dve, uc[:], ub[:, 0:TT], p["convw"][:, 0, c:c + 1], p["convb"][:, c:c + 1], ALU.mult, ALU.add)
                for k in range(1, 4):
                    stt(dve if k % 2 else pool, uc[:], ub[:, k:k + TT], p["convw"][:, k, c:c + 1], uc[:], ALU.mult, ALU.add)
                ucb = htmp.get()
                copy(act, ucb[:], uc[:])
                psr = psum.get(); psi = psum.get()
                mm(psr[:], V(slw, vw[:, c, :]), ucb[:], True, True)
                mm(psi[:], V(slw2, vw2[:, c, :]), ucb[:], True, True)
                htmp.put(ucb)
                rg = ftmp.get(); ig = ftmp.get()
                actf(rg[:], psr[:], AF.Sigmoid, bias=p["b_a"][:, c:c + 1])
                actf(ig[:], psi[:], AF.Sigmoid, bias=p["b_i"][:, c:c + 1])
                psum.put(psr); psum.put(psi)
                a_ = ftmp.get(); m_ = ftmp.get()
                actf(a_[:], rg[:], AF.Exp, scale=p["cA"][:, c:c + 1])
                actf(m_[:], rg[:], AF.Exp, scale=p["cA2"][:, c:c + 1])
                actf(m_[:], m_[:], AF.Sqrt, bias=1.0, scale=-1.0)
                tt(pool, ig[:], ig[:], uc[:], ALU.mult)
                tt(dve, ig[:], ig[:], m_[:], ALU.mult)
                scan(rg[:], a_[:], ig[:], s["hl"][:, c:c + 1])
                copy(dve, s["hl"][:, c:c + 1], rg[:, TT - 1:TT])
                psg = psum.get()
                proj_fm(psg, slg, vg, c * 128, 128)
                actf(m_[:], psg[:], AF.Gelu, bias=p["b_main"][:, 4 + c:5 + c])
                psum.put(psg)
                tt(pool, ya[c][:], rg[:], m_[:], ALU.mult)
                for b_ in (uc, rg, ig, a_, m_):
                    ftmp.put(b_)
                cx.tag = "mixC"
                yield
                cx.tag = "mixA"
            for s_ in (slu, slg, slw, slw2):
                slabs.put(s_)

            cx.tag = "mixC"

        yb = [htmp.get() for _ in range(4)]

        def genB():
            cx.tag = "mixB"
            slq, vq = load_slab(w_in_s, kview(w_in_s, l, 1024, 512))
            slv_, vv = load_slab(w_in_s, kview(w_in_s, l, 1536, 512))
            slo, vo = load_slab(w_in_s, kview(w_in_s, l, 2048, 512))
            sllr, vlr = load_slab(w_in_s, kview(w_in_s, l, 2560, 16))
            ps = psum.get()
            proj_fm(ps, sllr, vlr, 0, 16)
            lrb = htmp.get()
            actf(lrb[0:16, :], ps[0:16, :], AF.Identity, bias=p["b_lr"][:, 0:1])
            psum.put(ps); slabs.put(sllr)
            vtok = [htmp.get() for _ in range(4)]
            for n in range(4):
                ps = psum.get()
                for k in range(8):
                    mm(ps[:], cur["hb"][k][:, n * 128:(n + 1) * 128], V(slv_, vv[:, k, :]), k == 0, k == 7)
                tt(dve, vtok[n][:], ps[:], p["b_v"][:], ALU.add)
                psum.put(ps)
            slabs.put(slv_)
            cx.tag = "mixC"
            yield
            cx.tag = "mixB"
            for h in range(4):
                psq = psum.get(); psk = psum.get(); psx = psum.get()
                proj_fm(psq, slq, vq, h * 64, 64)
                proj_fm(psk, slq, vq, 256 + h * 64, 64)
                mm(psx[0:64, :], p["w_lr"][:, h * 64:(h + 1) * 64], lrb[0:16, :], True, True)
                qf = ftmp.get(); kf = ftmp.get(); sp_ = ftmp.get()
                actf(qf[0:64, :], psq[0:64, :], AF.Identity, bias=p["bq8"][:, h:h + 1], scale=0.125)
                actf(kf[0:64, :], psk[0:64, :], AF.Identity, bias=p["b_qk"][:, 4 + h:5 + h])
                actf(sp_[0:64, :], psx[0:64, :], AF.Exp, bias=p["nb_lr"][:, h:h + 1], scale=-1.0)
                actf(sp_[0:64, :], sp_[0:64, :], AF.Ln, bias=1.0)
                psum.put(psq); psum.put(psk); psum.put(psx)
                gn_ = ftmp.get()
                scan(gn_[0:64, :], cmask[0:64, :], sp_[0:64, :], 0.0)
                eg = ftmp.get()
                actf(eg[0:64, :], gn_[0:64, :], AF.Exp, scale=-1.0 / 16)
                actf(sp_[0:64, :], gn_[0:64, :], AF.Exp, scale=1.0 / 16)
                qd = htmp.get(); kib = htmp.get()
                tt(dve, qd[0:64, :], qf[0:64, :], eg[0:64, :], ALU.mult)
                tt(pool, kf[0:64, :], kf[0:64, :], sp_[0:64, :], ALU.mult)
                copy(act, kib[0:64, :], kf[0:64, :])
                for n in range(4):
                    ts(dve, gn_[0:64, n * 128:(n + 1) * 128], kf[0:64, n * 128:(n + 1) * 128],
                       eg[0:64, n * 128 + 127:n * 128 + 128], None, ALU.mult)
                psa = psum.get()
                for n in range(4):
                    mm(psa[:, n * 128:(n + 1) * 128], kib[0:64, n * 128:(n + 1) * 128], qd[0:64, n * 128:(n + 1) * 128], True, True)
                attb = htmp.get()
                tt(dve, attb[:], psa[:], mask4[:], ALU.mult)
                psum.put(psa)
                kend = htmp.get()
                pst = psum.get()
                for n in range(4):
                    transpose(pst[:, n * 64:(n + 1) * 64], gn_[0:64, n * 128:(n + 1) * 128], ident[0:64, 0:64])
                copy(act, kend[:, 0:256], pst[:, 0:256])
                psum.put(pst)
                pso = psum.get()
                S = s["S"][h]; Sb = s["Sb"][h]
                for n in range(4):
                    mm(pso[:, n * 128:(n + 1) * 128], vtok[n][:, h * 128:(h + 1) * 128], attb[:, n * 128:(n + 1) * 128], True, False)
                    mm(pso[:, n * 128:(n + 1) * 128], Sb[:], qd[0:64, n * 128:(n + 1) * 128], False, True)
                    psd = psum.get()
                    mm(psd[0:64, 0:128], kend[:, n * 64:(n + 1) * 64], vtok[n][:, h * 128:(h + 1) * 128], True, True)
                    stt(dve, S[:], S[:], eg[0:64, n * 128 + 127:n * 128 + 128], psd[0:64, 0:128], ALU.mult, ALU.add)
                    psum.put(psd)
                    copy(pool, Sb[:], S[:])
                for b_ in (qd, kib, attb, kend):
                    htmp.put(b_)
                for b_ in (qf, kf, sp_, gn_, eg):
                    ftmp.put(b_)
                of = ftmp.get(); sqb = htmp.get()
                actf(of[:], pso[:], AF.Identity, scale=p["gn"][:, 0:1])
                actf(sqb[:], pso[:], AF.Square)
                psum.put(pso)
                psm = psum.get()
                mm(psm[:], ones_v[:], sqb[:], True, True)
                htmp.put(sqb)
                rs_ = ftmp.get()
                rsqrt_eps(rs_[:], psm[:])
                psum.put(psm)
                tt(pool, of[:], of[:], rs_[:], ALU.mult)
                psg = psum.get()
                proj_fm(psg, slo, vo, h * 128, 128)
                actf(rs_[:], psg[:], AF.Silu, bias=p["b_main"][:, 16 + h:17 + h])
                psum.put(psg)
                tt(dve, yb[h][:], of[:], rs_[:], ALU.mult)
                ftmp.put(of); ftmp.put(rs_)
                cx.tag = "mixC"
                yield
                cx.tag = "mixB"
            for b_ in vtok:
                htmp.put(b_)
            htmp.put(lrb)
            slabs.put(slq); slabs.put(slo)

            cx.tag = "mixC"

        ys = [ya, yb, None]
        mg = []
        acc = []

        def merge_steps(n):
            tag0 = cx.tag
            cx.tag = "merge"
            if n == 0:
                acc.extend(ftmp.get() for _ in range(8))
            if n == 2:
                mg.extend(htmp.get() for _ in range(8))
            wsl, wv = load_slab(wbr_s, wbr_s.apv[l, n * 512:(n + 1) * 512].rearrange("(k p) c -> p k c", p=128))
            gsl = [None, None]
            cx.tag = tag0
            for k in range(8):
                tag0 = cx.tag
                cx.tag = "merge"
                if k % 4 == 0:
                    gsl[k // 4] = load_slab(w_in_s, kview(w_in_s, l, 3088 + n * 1024 + (k // 4) * 512, 512))
                psg = psum.get()
                proj_fm(psg, gsl[k // 4][0], gsl[k // 4][1], (k % 4) * 128, 128)
                gt = ftmp.get()
                actf(gt[:], psg[:], AF.Sigmoid, bias=p["b_gate"][:, n * 8 + k:n * 8 + k + 1])
                psum.put(psg)
                psb = psum.get()
                proj_fm(psb, wsl, wv, k * 128, 128, nk=4, rhs=ys[n])
                if n == 0:
                    tt(dve, acc[k][:], psb[:], gt[:], ALU.mult)
                else:
                    tt(dve, gt[:], psb[:], gt[:], ALU.mult)
                    if n == 1:
                        tt(pool, acc[k][:], acc[k][:], gt[:], ALU.add)
                    else:
                        tt(pool, mg[k][:], acc[k][:], gt[:], ALU.add)
                psum.put(psb)
                ftmp.put(gt)
                if k % 4 == 3:
                    slabs.put(gsl[k // 4][0])
                if k == 7:
                    slabs.put(wsl)
                cx.tag = tag0
                yield

        def others():
            yield from genA()
            gb_ = genB()
            m0 = merge_steps(0)
            for _ in gb_:
                yield
                for _k in range(2):
                    if next(m0, "end") != "end":
                        yield
            for _ in m0:
                yield
            yield from merge_steps(1)

        mgen = others()

        cx.tag = "mixC"
        yc = [htmp.get() for _ in range(4)]
        ys[2] = yc
        slu5, vu5 = load_slab(w_in_s, kview(w_in_s, l, 2576, 512))
        u5 = [ftmp.get() for _ in range(4)]; u5b = [htmp.get() for _ in range(4)]
        for c in range(4):
            ps = psum.get()
            proj_fm(ps, slu5, vu5, c * 128, 128)
            actf(u5[c][:], ps[:], AF.Identity, bias=p["b_s5"][:, c:c + 1])
            psum.put(ps)
            copy(pool, u5b[c][:], u5[c][:])
        slabs.put(slu5)
        s5v = s5w_s.apv[l].rearrange("p (kind j m) -> p kind j m", kind=5, j=16)
        for c in range(4):
            slBC = slabs.get()
            vBC = slBC.t[:, 0:2560].rearrange("p (kind j m) -> p kind j m", kind=5, j=4)
            dma(sp, V(slBC, vBC), V(s5w_s, s5v[:, :, 4 * c:4 * c + 4, :]))
            psy = psum.get()
            for jj in range(4):
                j = c * 4 + jj
                tb = tabs.get()
                dma(sp, tb[:], V(tab_s, tab_s.apv[l, j]))
                cs = tb[:, 0:TT]; sn = tb[:, TT:2 * TT]
                pbr = psum.get(); pbi = psum.get()
                mm(pbr[:], V(slBC, vBC[:, 0, jj, :]), u5b[c][:], True, True)
                mm(pbi[:], V(slBC, vBC[:, 1, jj, :]), u5b[c][:], True, True)
                br = ftmp.get(); bi = ftmp.get(); t1 = ftmp.get(); t2 = ftmp.get()
                copy(act, br[:], pbr[:]); copy(act, bi[:], pbi[:])
                psum.put(pbr); psum.put(pbi)
                tt(dve, t1[:], br[:], sn, ALU.mult)
                tt(pool, br[:], br[:], cs, ALU.mult)
                tt(dve, t2[:], bi[:], sn, ALU.mult)
                tt(pool, bi[:], bi[:], cs, ALU.mult)
                tt(pool, br[:], br[:], t2[:], ALU.add)
                tt(dve, bi[:], bi[:], t1[:], ALU.subtract)
                rho_b = V(p["rho"], p["rho"].t[:, j:j + 1].broadcast_to([128, TT]))
                scan(br[:], rho_b, br[:], s["zr"][:, j:j + 1])
                scan(bi[:], rho_b, bi[:], s["zi"][:, j:j + 1])
                actf(t1[:, 0:1], bi[:, TT - 1:TT], AF.Identity, scale=p["nsnT"][:, j:j + 1])
                actf(t1[:, 1:2], bi[:, TT - 1:TT], AF.Identity, scale=p["csT"][:, j:j + 1])
                actf(s["zr"][:, j:j + 1], br[:, TT - 1:TT], AF.Identity, scale=p["csT"][:, j:j + 1], bias=t1[:, 0:1])
                actf(s["zi"][:, j:j + 1], br[:, TT - 1:TT], AF.Identity, scale=p["snT"][:, j:j + 1], bias=t1[:, 1:2])
                p1 = htmp.get(); p2 = htmp.get(); p3 = htmp.get(); p4 = htmp.get()
                tt(pool, p1[:], br[:], cs, ALU.mult)
                tt(dve, p2[:], bi[:], sn, ALU.mult)
                tt(pool, p3[:], br[:], sn, ALU.mult)
                tt(dve, p4[:], bi[:], cs, ALU.mult)
                tabs.put(tb)
                mm(psy[:], V(slBC, vBC[:, 2, jj, :]), p1[:], jj == 0, False)
                mm(psy[:], V(slBC, vBC[:, 4, jj, :]), p2[:], False, False)
                mm(psy[:], V(slBC, vBC[:, 3, jj, :]), p3[:], False, False)
                mm(psy[:], V(slBC, vBC[:, 3, jj, :]), p4[:], False, jj == 3)
                for b_ in (br, bi, t1, t2):
                    ftmp.put(b_)
                for b_ in (p1, p2, p3, p4):
                    htmp.put(b_)
                next(mgen, None)
                if j < 10:
                    next(mgen, None)
            slabs.put(slBC)
            stt(dve, u5[c][:], u5[c][:], p["s5d"][:, c:c + 1], psy[:], ALU.mult, ALU.add)
            psum.put(psy)
            actf(u5[c][:], u5[c][:], AF.Gelu)
            copy(pool, u5b[c][:], u5[c][:])
        for _ in mgen:
            pass
        slG, vG = load_slab(wglu_s, kview(wglu_s, l, 0, 512))
        for c in range(4):
            ps = psum.get()
            proj_fm(ps, slG, vG, c * 128, 128, nk=4, rhs=u5b)
            sg_ = ftmp.get()
            actf(sg_[:], ps[:], AF.Sigmoid, bias=p["bglu"][:, c:c + 1])
            psum.put(ps)
            tt(dve, yc[c][:], u5[c][:], sg_[:], ALU.mult)
            ftmp.put(sg_)
        for c in range(4):
            ftmp.put(u5[c]); htmp.put(u5b[c])
        slabs.put(slG)
        for _ in merge_steps(2):
            pass
        cx.tag = "merge"
        for k in range(8):
            ftmp.put(acc[k])
        for b_ in ya + yb + yc:
            htmp.put(b_)
        for half in range(2):
            slm, vm = load_slab(wmix_s, kview(wmix_s, l, half * 512, 512))
            for kk in range(4):
                k = half * 4 + kk
                ps = psum.get()
                proj_fm(ps, slm, vm, kk * 128, 128, rhs=mg)
                stt(dve, rout[k][:], hcur[k][:], ALPHA, ps[:], ALU.mult, ALU.add)
                psum.put(ps)
            slabs.put(slm)
        for k in range(8):
            htmp.put(mg[k])

    def xattn(l, hcur, rout):
        cx.tag = "xattn"
        qT = [htmp.get() for _ in range(8)]
        for half in range(2):
            slq, vq = load_slab(wq_s, kview(wq_s, l, half * 512, 512))
            for kk in range(4):
                ps = psum.get()
                proj_fm(ps, slq, vq, kk * 128, 128)
                actf(qT[half * 4 + kk][:], ps[:], AF.Copy, scale=1.0 / 16)
                psum.put(ps)
            slabs.put(slq)
        slk, vk = load_slab(kt_s, kt_s.apv[l].rearrange("p (c m) -> p c m", m=256))
        slvv, vvv = load_slab(v_s, v_s.apv[l].rearrange("p (c m) -> p c m", m=1024))
        ob = [htmp.get() for _ in range(8)]
        for h in range(4):
            pT = [htmp.get(), htmp.get()]
            for mc in range(2):
                ps = psum.get()
                for hc in range(2):
                    mm(ps[:], V(slk, vk[:, h * 2 + hc, mc * 128:(mc + 1) * 128]), qT[h * 2 + hc][:], hc == 0, hc == 1)
                actf(pT[mc][:], ps[:], AF.Exp)
                psum.put(ps)
            psd = psum.get()
            for mc in range(2):
                mm(psd[:], ones_1[:], pT[mc][:], mc == 0, mc == 1)
            rd = ftmp.get()
            actf(rd[:], psd[:], AF.Ln)
            actf(rd[:], rd[:], AF.Exp, scale=-1.0)
            psum.put(psd)
            for hc in range(2):
                ps = psum.get()
                for mc in range(2):
                    mm(ps[:], V(slvv, vvv[:, mc, h * 256 + hc * 128:h * 256 + (hc + 1) * 128]), pT[mc][:], mc == 0, mc == 1)
                tt(dve, ob[h * 2 + hc][:], ps[:], rd[:], ALU.mult)
                psum.put(ps)
            ftmp.put(rd); htmp.put(pT[0]); htmp.put(pT[1])
        slabs.put(slk); slabs.put(slvv)
        for b_ in qT:
            htmp.put(b_)
        for half in range(2):
            slo, vo = load_slab(wo_s, kview(wo_s, l, half * 512, 512))
            for kk in range(4):
                k = half * 4 + kk
                ps = psum.get()
                proj_fm(ps, slo, vo, kk * 128, 128, rhs=ob)
                stt(dve, rout[k][:], hcur[k][:], ALPHA, ps[:], ALU.mult, ALU.add)
                psum.put(ps)
            slabs.put(slo)
        for b_ in ob:
            htmp.put(b_)

    def ffn(l, hcur, rout):
        cx.tag = "ffn"
        ab = [htmp.get() for _ in range(22)]
        for q in range(6):
            w = 512 if q < 5 else 256
            slg, vg = load_slab(wgu_s, kview(wgu_s, l, q * 512, w))
            slu, vu = load_slab(wgu_s, kview(wgu_s, l, DFF + q * 512, w))
            for kk in range(w // 128):
                c = q * 4 + kk
                psg = psum.get(); psu = psum.get()
                proj_fm(psg, slg, vg, kk * 128, 128)
                proj_fm(psu, slu, vu, kk * 128, 128)
                sg_ = ftmp.get()
                actf(sg_[:], psg[:], AF.Silu)
                tt(dve, ab[c][:], psu[:], sg_[:], ALU.mult)
                psum.put(psg); psum.put(psu); ftmp.put(sg_)
            slabs.put(slg); slabs.put(slu)
        for k in range(8):
            sld, vd = load_slab(wdn_s, kview(wdn_s, l, k * 128, 128))
            ps = psum.get()
            proj_fm(ps, sld, vd, 0, 128, nk=22, rhs=ab)
            stt(dve, rout[k][:], hcur[k][:], ALPHA, ps[:], ALU.mult, ALU.add)
            psum.put(ps)
            slabs.put(sld)
        for b_ in ab:
            htmp.put(b_)

    out_toks = []
    ostg = [cx.sbuf([128, 512], F32, f"ostg{i}") for i in range(2)]
    nx = [0]; no = [0]
    for it in range(n_tiles + LAG):
        hcur = hS[it % 2]; cur["hb"] = hbs[it % 2]
        cx.itn = it
        cx.tag = "entry"
        if it < n_tiles:
            for n in range(4):
                r0 = it * TT + n * 128
                xh = [ftmp.get(), ftmp.get()]
                st6 = ftmp.get()
                for hf in range(2):
                    dma(sp, xh[hf][:], V(x_d, x_d.apv[r0:r0 + 128, hf * 512:(hf + 1) * 512]))
                    cx.op(dve, lambda e, hf=hf, xb=xh[hf], st6=st6: e.bn_stats(out=st6.t[:, hf * 6:(hf + 1) * 6], in_=xb.t[:]), [xh[hf]], [st6], 0.65)
                cx.op(dve, lambda e, st6=st6: e.bn_aggr(out=st6.t[:, 16:18], in_=st6.t[:, 0:12]), [st6], [st6], 0.2)
                rsqrt_eps(st6[:, 18:19], st6[:, 17:18])
                for hf in range(2):
                    ts(dve, xh[hf][:], xh[hf][:], st6[:, 16:17], st6[:, 18:19], ALU.subtract, ALU.mult)
                ftmp.put(st6)
                for k in range(8):
                    ps = psum.get()
                    transpose(ps[:, 0:128], xh[k // 4][:, (k % 4) * 128:(k % 4 + 1) * 128], ident[:])
                    actf(hcur[k][:, n * 128:(n + 1) * 128], ps[:, 0:128], AF.Identity, bias=b_in_ln[:, k:k + 1], scale=g_in[:, k:k + 1])
                    psum.put(ps)
                ftmp.put(xh[0]); ftmp.put(xh[1])
        if it >= LAG:
            cx.tag = "recv"
            gb = gat_d[(it - LAG) % 2]
            if LAG == 1:
                cc_gather(snd_d[(it - 1) % 3], gb)
            for k in range(8):
                rk = ftmp.get()
                dma(sp, rk[:], V(gb, gb.apv[0:128, k * TT:(k + 1) * TT]))
                if it < n_tiles:
                    stt(dve, hcur[k][:], rk[:], nsel[:, 0:1], hcur[k][:], ALU.mult, ALU.add)
                else:
                    ts(dve, hcur[k][:], rk[:], nsel[:, 0:1], None, ALU.mult)
                ftmp.put(rk)
        for k in range(8):
            copy(act, cur["hb"][k][:], hcur[k][:])
        stage = 0
        for l in range(NL):
            p = LP[l]
            for (fn, g_, b_) in ((lambda a, b: mixer(l, a, b, it == 0), p["ln1g"], p["ln1b"]),
                                 (lambda a, b: xattn(l, a, b), p["ln2g"], p["ln2b"]),
                                 (lambda a, b: ffn(l, a, b), p["ln3g"], p["ln3b"])):
                stage += 1
                if stage > n_stage - 1:
                    continue
                fn(hcur, hcur)
                layer_norm(hcur, cur["hb"], g_, b_)
        cx.tag = "out"
        if it < n_tiles:
            sb_ = snd_d[it % 3]
            grp = {}
            for k in range(8):
                dma(sp, V(sb_, sb_.apv[:, k * TT:(k + 1) * TT]), hcur[k][:], sembuf=sb_, group=grp)
            if LAG > 1:
                cc_gather(sb_, gat_d[it % 2])
        if it >= LAG:
            for n in range(4):
                for hf in range(2):
                    ob_ = ostg[no[0] % 2]; no[0] += 1
                    for kk in range(4):
                        k = hf * 4 + kk
                        ps = psum.get()
                        transpose(ps[:, 0:128], hcur[k][:, n * 128:(n + 1) * 128], ident[:])
                        copy(act if k % 2 else dve, ob_[:, kk * 128:(kk + 1) * 128], ps[:, 0:128])
                        psum.put(ps)
                    r0 = (it - LAG) * TT + n * 128
                    out_toks.append(dma(sp, V(out_d, out_d.apv[r0:r0 + 128, hf * 512:(hf + 1) * 512]), ob_[:], sembuf=ob_, track_out=False))
        if it == LAG - 1:
            cx.tag = "reset"
            for l in range(NL):
                s_ = ST[l]
                for c in range(4):
                    ts(dve, s_["ubuf"][c][:, TT:TT + 3], s_["ubuf"][c][:, TT:TT + 3], sel[:, 0:1], None, ALU.mult)
                ts(dve, s_["hl"][:], s_["hl"][:], sel[:, 0:1], None, ALU.mult)
                for h in range(4):
                    ts(dve, s_["S"][h][:], s_["S"][h][:], sel[0:64, 0:1], None, ALU.mult)
                    ts(dve, s_["Sb"][h][:], s_["Sb"][h][:], sel[0:64, 0:1], None, ALU.mult)
                ts(dve, s_["zr"][:], s_["zr"][:], sel[:, 0:1], None, ALU.mult)
                ts(dve, s_["zi"][:], s_["zi"][:], sel[:, 0:1], None, ALU.mult)
    cx.final_wait(sp, out_toks[-2:])


_WNAMES = ["ln_in_g", "ln_in_b", "w_in", "b_in", "lru_conv_w", "lru_conv_b", "lru_w_a", "lru_b_a", "lru_w_i", "lru_b_i",
           "lru_lambda", "gla_w_lr", "gla_b_lr", "gla_norm_g", "s5_lam_re", "s5_lam_im", "s5_log_dt", "s5_b_re", "s5_b_im",
           "s5_c_re", "s5_c_im", "s5_d", "s5_w_glu", "s5_b_glu", "w_branch", "w_mix_out", "ln1_g", "ln1_b", "xa_w_q",
           "xa_w_kv", "xa_w_o", "ln2_g", "ln2_b", "ffn_w_gu", "ffn_w_down", "ln3_g", "ln3_b"]


def kernel(**inputs):
    x = np.ascontiguousarray(inputs["x"], dtype=np.float32)
    mem = np.ascontiguousarray(inputs["mem"], dtype=np.float32)
    B, L, _ = x.shape
    nc = build(n_tiles=L // TT, ncores=2 * B)
    per_layer = []
    for l in range(DEPTH):
        w = {}
        for k in _WNAMES:
            a = np.asarray(inputs[k], dtype=np.float32)
            w[k] = np.ascontiguousarray(a) if k in ("ln_in_g", "ln_in_b") else np.ascontiguousarray(a[l:l + 1])
        per_layer.append(w)
    zeros_x = np.zeros((L, D), np.float32)
    in_maps = []
    for b in range(B):
        for l in range(DEPTH):
            m = dict(per_layer[l])
            m["x"] = x[b] if l == 0 else zeros_x
            m["mem"] = mem[b]
            m["role"] = np.full((128, 1), 1.0 if l == 0 else 0.0, np.float32)
            in_maps.append(m)
    res = run_bass_kernel_spmd(nc, in_maps, core_ids=list(range(2 * B)))
    return np.stack([res.results[2 * b + 1]["out"] for b in range(B)], axis=0)
```
